# Optimizing a Trainium2 kernel written in Bass

```python
import jax
import jax.numpy as jnp
from jax import lax
import numpy as np


D_MODEL = 1024
BATCH = 4
SEQ = 8192
DEPTH = 4
DEC_BATCH = 32
DEC_SEQ = 16
PAST_LEN = 4096

CHUNK = 64
RET_HEADS = 4
RET_KEY_DIM = 128
RET_VAL_DIM = 128
RET_QK_WIDTH = RET_HEADS * RET_KEY_DIM
RET_V_WIDTH = RET_HEADS * RET_VAL_DIM
GMLP_CHUNK = 128
GMLP_GROUPS = 4
GMLP_WIDTH = 512
GMLP_GROUP_DIM = GMLP_WIDTH // GMLP_GROUPS
D_FF = 2816
ROPE_BASE = 10000.0
EPS = 1e-6
IN_COLS = 2 * RET_QK_WIDTH + 2 * RET_V_WIDTH + 2 * GMLP_WIDTH + 2 * D_MODEL

kernel_name = 'retention_gmlp_macaron_streaming_step'


def rms_norm(x, w):
    xf = x.astype(jnp.float32)
    y = xf * lax.rsqrt(jnp.mean(xf * xf, axis=-1, keepdims=True) + EPS)
    return (y * w.astype(jnp.float32)).astype(x.dtype)


def layer_norm(x, w, b):
    xf = x.astype(jnp.float32)
    xc = xf - jnp.mean(xf, axis=-1, keepdims=True)
    y = xc * lax.rsqrt(jnp.mean(xc * xc, axis=-1, keepdims=True) + EPS)
    return (y * w.astype(jnp.float32) + b.astype(jnp.float32)).astype(x.dtype)


def swiglu_ffn(x, w_gu, w_down):
    a, b = jnp.split(x @ w_gu, 2, axis=-1)
    return (jax.nn.silu(a) * b) @ w_down


def rotary(x, pos):
    half = x.shape[-1] // 2
    freqs = ROPE_BASE ** (-jnp.arange(half, dtype=jnp.float32) / half)
    ang = pos[:, None] * freqs[None, :]
    cos = jnp.cos(ang)[None, :, None, :]
    sin = jnp.sin(ang)[None, :, None, :]
    x1, x2 = x[..., :half], x[..., half:]
    return jnp.concatenate([x1 * cos - x2 * sin, x1 * sin + x2 * cos], axis=-1)


def retention_log_decay():
    return jnp.log1p(-jnp.exp2(-5.0 - jnp.arange(RET_HEADS, dtype=jnp.float32)))


def multiscale_retention(q, k, v, s0):
    bsz, length, n_heads, dk = q.shape
    dv = v.shape[-1]
    c = min(CHUNK, length)
    n = length // c
    log_g = retention_log_decay()
    idx = jnp.arange(c, dtype=jnp.float32)
    diff = idx[:, None] - idx[None, :]
    decay = jnp.exp(jnp.where((diff >= 0)[None], diff[None] * log_g[:, None, None], -jnp.inf))
    qc = q.reshape(bsz, n, c, n_heads, dk)
    kc = k.reshape(bsz, n, c, n_heads, dk)
    vc = v.reshape(bsz, n, c, n_heads, dv)
    scores = jnp.einsum('bnihd,bnjhd->bnhij', qc, kc) * decay
    intra = jnp.einsum('bnhij,bnjhe->bnihe', scores, vc)
    k_decay = jnp.exp((c - 1.0 - idx)[:, None] * log_g[None, :])
    kv = jnp.einsum('bnjhd,jh,bnjhe->nbhde', kc, k_decay, vc)
    block_decay = jnp.exp(c * log_g)[None, :, None, None]

    def step(s, kv_n):
        return block_decay * s + kv_n, s

    s_final, s_prev = lax.scan(step, s0, kv)
    q_decay = jnp.exp((idx + 1.0)[:, None] * log_g[None, :])
    cross = jnp.einsum('bnihd,ih,nbhde->bnihe', qc, q_decay, s_prev)
    return (intra + cross).reshape(bsz, length, n_heads, dv), s_final


def spatial_gating(u, v_n, ws, bs):
    bsz, length, _ = v_n.shape
    c = min(GMLP_CHUNK, length)
    n = length // c
    vg = v_n.reshape(bsz, n, c, GMLP_GROUPS, GMLP_GROUP_DIM)
    mask = jnp.tril(jnp.ones((c, c), dtype=bool))
    w = jnp.where(mask[None], ws[:, :c, :c], 0.0)
    s = jnp.einsum('gij,bnjgd->bnigd', w, vg) + bs[:, :c].T[None, None, :, :, None]
    return u * s.reshape(bsz, length, GMLP_WIDTH)


def token_mixing(h, pos, s0, w_in, ret_gn_w, gmlp_ln_w, gmlp_ln_b, gmlp_ws, gmlp_bs, w_br_ret, w_br_mlp):
    bsz, length, _ = h.shape
    proj = h @ w_in
    sizes = (RET_QK_WIDTH, RET_QK_WIDTH, RET_V_WIDTH, RET_V_WIDTH, GMLP_WIDTH, GMLP_WIDTH, D_MODEL)
    cuts = [int(c) for c in np.cumsum(sizes)]
    q, k, v_r, g_r, u, v_m, a_ret, a_mlp = jnp.split(proj, cuts, axis=-1)
    f32 = jnp.float32
    qh = rotary(q.reshape(bsz, length, RET_HEADS, RET_KEY_DIM).astype(f32), pos)
    kh = rotary(k.reshape(bsz, length, RET_HEADS, RET_KEY_DIM).astype(f32), pos) * (RET_KEY_DIM ** -0.5)
    vh = v_r.reshape(bsz, length, RET_HEADS, RET_VAL_DIM).astype(f32)
    o, s_new = multiscale_retention(qh, kh, vh, s0.astype(f32))
    oc = o - jnp.mean(o, axis=-1, keepdims=True)
    o = oc * lax.rsqrt(jnp.mean(oc * oc, axis=-1, keepdims=True) + EPS)
    o = o.reshape(bsz, length, RET_V_WIDTH) * ret_gn_w.astype(f32)
    y_ret = (o * jax.nn.silu(g_r.astype(f32))).astype(h.dtype) @ w_br_ret
    v_n = layer_norm(v_m, gmlp_ln_w, gmlp_ln_b)
    y_mlp = spatial_gating(u, v_n, gmlp_ws, gmlp_bs) @ w_br_mlp
    merged = jax.nn.sigmoid(a_ret) * y_ret + jax.nn.sigmoid(a_mlp) * y_mlp
    return merged, s_new.astype(h.dtype), v_n


def setup_inputs(seed: int = 0) -> dict:
    key = jax.random.key(seed)
    ks = jax.random.split(key, 24)
    f32 = jnp.float32
    nrm = lambda k, shape, scale: jax.random.normal(k, shape, f32) * scale
    gain = lambda k, shape: 1.0 + 0.02 * jax.random.normal(k, shape, f32)
    return {
        'x_prompt': nrm(ks[0], (BATCH, SEQ, D_MODEL), 1.0),
        'x_sample': nrm(ks[1], (DEC_BATCH, DEC_SEQ, D_MODEL), 1.0),
        'state_ret': nrm(ks[2], (DEPTH, DEC_BATCH, RET_HEADS, RET_KEY_DIM, RET_VAL_DIM), 0.3),
        'ffn1_norm': gain(ks[3], (DEPTH, D_MODEL)),
        'ffn1_w_gu': nrm(ks[4], (DEPTH, D_MODEL, 2 * D_FF), D_MODEL ** -0.5),
        'ffn1_w_down': nrm(ks[5], (DEPTH, D_FF, D_MODEL), D_FF ** -0.5),
        'mix_norm': gain(ks[6], (DEPTH, D_MODEL)),
        'w_in': nrm(ks[7], (DEPTH, D_MODEL, IN_COLS), D_MODEL ** -0.5),
        'ret_gn_w': gain(ks[8], (DEPTH, RET_V_WIDTH)),
        'gmlp_ln_w': gain(ks[9], (DEPTH, GMLP_WIDTH)),
        'gmlp_ln_b': nrm(ks[10], (DEPTH, GMLP_WIDTH), 0.02),
        'gmlp_ws': nrm(ks[11], (DEPTH, GMLP_GROUPS, GMLP_CHUNK, GMLP_CHUNK), GMLP_CHUNK ** -0.5),
        'gmlp_bs': gain(ks[12], (DEPTH, GMLP_GROUPS, GMLP_CHUNK)),
        'w_br_ret': nrm(ks[13], (DEPTH, RET_V_WIDTH, D_MODEL), RET_V_WIDTH ** -0.5),
        'w_br_mlp': nrm(ks[14], (DEPTH, GMLP_WIDTH, D_MODEL), GMLP_WIDTH ** -0.5),
        'w_out': nrm(ks[15], (DEPTH, D_MODEL, D_MODEL), D_MODEL ** -0.5),
        'ffn2_norm': gain(ks[16], (DEPTH, D_MODEL)),
        'ffn2_w_gu': nrm(ks[17], (DEPTH, D_MODEL, 2 * D_FF), D_MODEL ** -0.5),
        'ffn2_w_down': nrm(ks[18], (DEPTH, D_FF, D_MODEL), D_FF ** -0.5),
        'final_norm': gain(ks[19], (D_MODEL,)),
    }


def reference(x_prompt, x_sample, state_ret, ffn1_norm, ffn1_w_gu, ffn1_w_down, mix_norm, w_in, ret_gn_w, gmlp_ln_w, gmlp_ln_b, gmlp_ws, gmlp_bs, w_br_ret, w_br_mlp, w_out, ffn2_norm, ffn2_w_gu, ffn2_w_down, final_norm):
    def layer(x, pos, s0, l):
        x = x + 0.5 * swiglu_ffn(rms_norm(x, ffn1_norm[l]), ffn1_w_gu[l], ffn1_w_down[l])
        merged, s_new, v_rows = token_mixing(rms_norm(x, mix_norm[l]), pos, s0, w_in[l], ret_gn_w[l], gmlp_ln_w[l], gmlp_ln_b[l], gmlp_ws[l], gmlp_bs[l], w_br_ret[l], w_br_mlp[l])
        x = x + merged @ w_out[l]
        x = x + 0.5 * swiglu_ffn(rms_norm(x, ffn2_norm[l]), ffn2_w_gu[l], ffn2_w_down[l])
        return x, s_new, v_rows

    bp, lp = x_prompt.shape[0], x_prompt.shape[1]
    ls = x_sample.shape[1]
    pos_p = jnp.arange(lp, dtype=jnp.float32)
    pos_s = PAST_LEN + jnp.arange(ls, dtype=jnp.float32)
    s0_p = jnp.zeros((bp, RET_HEADS, RET_KEY_DIM, RET_VAL_DIM), jnp.float32)
    xp, xs = x_prompt, x_sample
    sp_list, ss_list, vs_list = [], [], []
    for l in range(DEPTH):
        xp, sp, _ = layer(xp, pos_p, s0_p, l)
        xs, ss, vs = layer(xs, pos_s, state_ret[l], l)
        sp_list.append(sp)
        ss_list.append(ss)
        vs_list.append(vs)
    y_prompt = rms_norm(xp, final_norm)
    y_sample = rms_norm(xs, final_norm)
    state_ret_prompt = jnp.stack(sp_list)
    state_ret_sample = jnp.stack(ss_list)
    gmlp_v_sample = jnp.stack(vs_list)
    return (y_prompt, y_sample, state_ret_prompt, state_ret_sample, gmlp_v_sample)
```

```python
import contextlib
import numpy as np
import concourse.bass as bass
import concourse.mybir as mybir
from concourse.bass_utils import run_bass_kernel_spmd

F32 = mybir.dt.float32
BF16 = mybir.dt.bfloat16
ALU = mybir.AluOpType
AF = mybir.ActivationFunctionType
_ESZ = {F32: 4, BF16: 2}

D = 1024
DFF = 2816
NJ = 22
NH = 4
TT = 512
EPS = 1e-6
WTOT = 192512
NS_RING = 5
SLOT_E = 4096


class Sem:
    __slots__ = ("name", "step", "h", "total")

    def __init__(self, name, step):
        self.name = name
        self.step = step
        self.h = None
        self.total = 0


class Op:
    __slots__ = ("q", "csem", "fn", "deps", "inc", "cnt", "sig", "idx")


def _region(ap):
    t = ap.tensor
    name = t.name
    sp = str(ap.space).upper()
    if "DRAM" in sp or "HBM" in sp:
        return name, 0, 1 << 60
    if "PSUM" in sp:
        return name.split("_bitcast")[0], 0, 2048
    F = 1
    for s in list(t.shape)[1:]:
        F *= int(s)
    esz = _ESZ.get(t.dtype, 4)
    off = int(ap.offset) % F
    ext = 0
    for (st, cn) in list(ap.ap)[1:]:
        ext += (int(cn) - 1) * abs(int(st))
    return name, off * esz, (off + ext + 1) * esz


class Prog:
    QUEUES = ("pe", "act", "dve", "pool", "sp")

    def __init__(self, nc):
        self.nc = nc
        self.ops = []
        self.qops = {q: [] for q in self.QUEUES}
        self.acc = {}
        self.esem = {q: Sem("s_" + q, 1) for q in ("pe", "act", "dve", "pool")}
        self.dsems = []
        self.pe_pending = []

    def dma_sem(self, name):
        s = Sem(name, 16)
        self.dsems.append(s)
        return s

    def rec(self, q, fn, reads, writes, csem=None, inc=True):
        op = Op()
        op.q = q
        op.fn = fn
        op.csem = csem if csem is not None else self.esem[q]
        op.inc = inc
        op.cnt = None
        op.sig = None
        idx = len(self.ops)
        op.idx = idx
        deps = set()
        csid = id(op.csem)
        for ap, is_w in [(r, False) for r in reads] + [(w, True) for w in writes]:
            name, lo, hi = _region(ap)
            d = self.acc.get(name)
            if d is None:
                d = {}
                self.acc[name] = d
            dead = []
            for key, j in d.items():
                l2, h2, cs2, w2 = key
                if j == idx:
                    continue
                if l2 < hi and lo < h2 and (is_w or w2):
                    deps.add(j)
                    if is_w and lo <= l2 and h2 <= hi:
                        dead.append(key)
            for k in dead:
                del d[k]
            d[(lo, hi, csid, is_w)] = idx
        real = set()
        for j in deps:
            o = self.ops[j]
            if q == "pe" and o.q == "pe":
                continue
            if not o.inc:
                if o.sig is None:
                    o.inc = True
                    k = self.pe_pending.index(o)
                    for p in self.pe_pending[: k + 1]:
                        p.sig = o
                    del self.pe_pending[: k + 1]
                real.add(o.sig.idx)
            else:
                real.add(j)
        op.deps = real
        self.ops.append(op)
        self.qops[q].append(op)
        if q == "pe":
            if inc:
                for p in self.pe_pending:
                    p.sig = op
                self.pe_pending = []
                op.sig = op
            else:
                self.pe_pending.append(op)
        else:
            op.sig = op
        return op

    def mm(self, out, lhsT, rhs, start=True, stop=True, inc=None):
        if inc is None:
            inc = stop
        return self.rec("pe", lambda e: e.matmul(out, lhsT, rhs, start=start, stop=stop),
                        [lhsT, rhs], [out], inc=inc)

    def tr(self, out, in_, ident, inc=True):
        return self.rec("pe", lambda e: e.transpose(out, in_, ident), [in_, ident], [out], inc=inc)

    def act(self, out, in_, func, bias=0.0, scale=1.0):
        rd = [in_]
        if not isinstance(bias, (int, float)):
            rd.append(bias)
        if not isinstance(scale, (int, float)):
            rd.append(scale)
        return self.rec("act", lambda e: e.activation(out, in_, func, bias=bias, scale=scale), rd, [out])

    def tt(self, q, out, in0, in1, op):
        return self.rec(q, lambda e: e.tensor_tensor(out, in0, in1, op), [in0, in1], [out])

    def ts(self, q, out, in0, s1, s2, op0, op1=ALU.bypass):
        rd = [in0]
        for s in (s1, s2):
            if s is not None and not isinstance(s, (int, float)):
                rd.append(s)
        if s2 is None:
            return self.rec(q, lambda e: e.tensor_scalar(out, in0, s1, None, op0), rd, [out])
        return self.rec(q, lambda e: e.tensor_scalar(out, in0, s1, s2, op0, op1), rd, [out])

    def stt(self, q, out, in0, scalar, in1, op0, op1):
        rd = [in0, in1]
        if not isinstance(scalar, (int, float)):
            rd.append(scalar)
        return self.rec(q, lambda e: e.scalar_tensor_tensor(out, in0, scalar, in1, op0, op1), rd, [out])

    def copy(self, q, out, in_):
        if q == "act":
            return self.rec(q, lambda e: e.activation(out, in_, AF.Copy), [in_], [out])
        return self.rec(q, lambda e: e.tensor_copy(out, in_), [in_], [out])

    def memset(self, q, out, val):
        return self.rec(q, lambda e: e.memset(out, val), [], [out])

    def recip(self, out, in_):
        return self.rec("dve", lambda e: e.reciprocal(out, in_), [in_], [out])

    def dma(self, out, in_, sem, q="sp"):
        return self.rec(q, lambda e: e.dma_start(out=out, in_=in_), [in_], [out], csem=sem)

    def emit(self):
        nc = self.nc
        for op in self.ops:
            if op.inc:
                op.csem.total += op.csem.step
                op.cnt = op.csem.total
        allsems = list(self.esem.values()) + self.dsems
        with contextlib.ExitStack() as st:
            for s in allsems:
                s.h = st.enter_context(nc.semaphore(s.name))
            block = st.enter_context(nc.Block())
            ops = self.ops

            def run(q, e):
                seen = {}
                for op in self.qops[q]:
                    need = {}
                    for j in op.deps:
                        o = ops[j]
                        cs = o.csem
                        c = o.cnt
                        if need.get(cs, 0) < c:
                            need[cs] = c
                    for cs, c in need.items():
                        if seen.get(cs, 0) >= c:
                            continue
                        seen[cs] = c
                        e.wait_ge(cs.h, c)
                    ins = op.fn(e)
                    if op.inc:
                        ins.then_inc(op.csem.h, op.csem.step)
                if q == "sp":
                    for s in self.dsems:
                        if s.total > 0:
                            e.wait_ge(s.h, s.total)
                    for s in self.esem.values():
                        if s.total > 0:
                            e.wait_ge(s.h, s.total)

            mp = {"pe": block.tensor, "act": block.scalar, "dve": block.vector,
                  "pool": block.gpsimd, "sp": block.sync}
            for q in self.QUEUES:
                if q == "sp" or self.qops[q]:
                    mp[q](lambda e, q=q: run(q, e))


def layer_pieces():
    pcs = []

    def ffn(tag):
        for pc in range(11):
            bl = []
            for kc in range(8):
                for jj in range(2):
                    j = 2 * pc + jj
                    bl.append((tag + "_w_gu", kc * 128, j * 128, 128))
                    bl.append((tag + "_w_gu", kc * 128, DFF + j * 128, 128))
            pcs.append(("gu", bl))
        for m in range(8):
            pcs.append(("down", [(tag + "_w_down", j * 128, m * 128, 128) for j in range(NJ)]))

    ffn("ffn1")
    for nm, c0 in (("q", 0), ("k", 512), ("vr", 1024), ("vm", 2560), ("gr", 1536), ("u", 2048)):
        pcs.append((nm, [("w_in", kc * 128, c0, 512) for kc in range(8)]))
    for m in range(8):
        bl = [("w_in", kc * 128, 3072 + m * 128, 128) for kc in range(8)]
        bl += [("w_in", kc * 128, 4096 + m * 128, 128) for kc in range(8)]
        bl += [("w_br_ret", kc * 128, m * 128, 128) for kc in range(4)]
        bl += [("w_br_mlp", kc * 128, m * 128, 128) for kc in range(4)]
        pcs.append(("merge", bl))
    for hf in range(2):
        pcs.append(("wout", [("w_out", kc * 128, hf * 512, 512) for kc in range(8)]))
    ffn("ffn2")
    return pcs


_PIECES = layer_pieces()
_PIECE_E = [sum(b[3] for b in bl) for (_, bl) in _PIECES]
_PIECE_OFF = [int(x) for x in np.concatenate([[0], np.cumsum(_PIECE_E)[:-1]])]
assert sum(_PIECE_E) == WTOT and max(_PIECE_E) <= SLOT_E
_GRP_OF = []
_GRPS = []
_g0, _gn = 0, 0
for _pi, _e in enumerate(_PIECE_E):
    if False:
        _GRPS.append((_g0, _gn))
        _g0, _gn = _g0 + _gn, 0
    _GRP_OF.append(len(_GRPS))
    _gn += _e
_GRPS.append((_g0, _gn))


def pack_weights(w, l):
    cols = []
    for (_, bl) in _PIECES:
        for (key, r0, c0, n) in bl:
            cols.append(w[key][l][r0:r0 + 128, c0:c0 + n])
    return np.ascontiguousarray(np.concatenate(cols, axis=1), dtype=np.float32)


C_ID = 0
C_MP = 128
C_MS = 256
C_MPN = 384
C_MSN = 512
C_GNEG = 640
C_KDEC = 648
C_SEQM = 656
C_GPOS = 664
C_VEC = 664 + 1024


def build_consts(nl, w, m_a=1.0):
    gam = (1.0 - 2.0 ** (-5.0 - np.arange(NH))).astype(np.float64)
    p = np.arange(128)
    nvec = nl * 3 * 8 + 8 + nl * 4 + 2
    c = np.zeros((128, C_VEC + nvec), np.float32)
    c[:, C_ID:C_ID + 128] = np.eye(128)
    jj, ii = np.meshgrid(p, p, indexing="ij")
    mp = (ii >= jj)
    ms = mp & ((ii // 16) == (jj // 16))
    c[:, C_MP:C_MP + 128] = mp
    c[:, C_MS:C_MS + 128] = ms
    c[:, C_MPN:C_MPN + 128] = mp.T
    c[:, C_MSN:C_MSN + 128] = ms.T
    sc = 128.0 ** -0.5
    for h in range(NH):
        c[:, C_GNEG + h] = sc * gam[h] ** (-(p + 1.0))
        c[:, C_GNEG + 4 + h] = sc * gam[h] ** (-((p % 16) + 1.0))
        c[:, C_KDEC + h] = sc * gam[h] ** (127.0 - p)
        c[:, C_KDEC + 4 + h] = sc * gam[h] ** (15.0 - (p % 16))
        c[:, C_GPOS + h * 128:C_GPOS + (h + 1) * 128] = (gam[h] ** (p + 1.0))[None, :]
        c[:, C_GPOS + 512 + h * 128:C_GPOS + 512 + (h + 1) * 128] = (gam[h] ** ((p % 16) + 1.0))[None, :]
    for s in range(8):
        c[:, C_SEQM + s] = (p // 16) == s
    o = C_VEC
    for l in range(nl):
        for i, key in enumerate(("ffn1_norm", "mix_norm", "ffn2_norm")):
            c[:, o:o + 8] = w[key][l].reshape(8, 128).T
            o += 8
    c[:, o:o + 8] = w["final_norm"].reshape(8, 128).T
    o += 8
    for l in range(nl):
        c[:, o:o + 4] = w["ret_gn_w"][l].reshape(4, 128).T
        o += 4
    c[:, o] = m_a
    c[:, o + 1] = 1.0 - m_a
    return c


def gblk(kind):
    gam = 1.0 - 2.0 ** (-5.0 - np.arange(NH))
    return [float(g ** (128.0 if kind == 0 else 16.0)) for g in gam]


def build_cstab(positions):
    freqs = (np.float32(10000.0) ** (-np.arange(64, dtype=np.float32) / np.float32(64))).astype(np.float32)
    ang = positions.astype(np.float32)[..., None] * freqs[None, None, :]
    cs = np.stack([np.cos(ang), np.sin(ang)], axis=1).astype(np.float32)
    nt = positions.shape[0]
    cs = cs.reshape(nt, 2, 4, 128, 64).transpose(0, 3, 1, 2, 4)
    return np.ascontiguousarray(cs)


def build_program(npt, nl, ns=1, pipelined=False, groups=None, do_cast=True,
                  stages=("gate", "ffn1", "mix", "mC", "mD", "mE", "mF", "ffn2")):
    nc = bass.Bass("TRN2", target_bir_lowering=False)
    P = Prog(nc)
    has_sample = ns > 0
    nt = npt + ns
    ntok = nt * TT
    nvec = nl * 3 * 8 + 8 + nl * 4 + 2
    NC = C_VEC + nvec
    C_MA = C_VEC + nvec - 2

    def din(name, shape, dt=F32):
        return nc.dram_tensor(name, shape, dt, kind="ExternalInput").ap()

    def dout(name, shape):
        return nc.dram_tensor(name, shape, F32, kind="ExternalOutput").ap()

    xin = din("xin", [ntok, D])
    wf32 = din("wf32", [nl, 128, WTOT])
    cst = din("cst", [128, NC])
    cstab = din("cstab", [nt, 128, 2 * 4 * 64])
    lnwb = din("lnwb", [nl, 2 * 512])
    gws = din("gws", [nl, 4, 128, 128])
    gbs = din("gbs", [nl, 4, 128])
    sret = din("sret", [nl, 32, 4, 128, 128])
    yout = dout("yout", [ntok, D])
    spo = dout("spo", [2, nl, 4, 128, 128])
    sso = dout("sso", [max(ns, 1), nl, 32, 4, 128, 128])
    gvo = dout("gvo", [max(ns, 1), nl, 512, 512])
    send = nc.dram_tensor("sendbuf", [128, 8 * TT], F32).ap()
    gath = nc.dram_tensor("gathbuf", [256, 8 * TT], F32).ap()
    wbf = [[nc.dram_tensor("wbf%d_%d" % (l, gi), [128, gn], BF16).ap() for gi, (g0, gn) in enumerate(_GRPS)]
           for l in range(nl)]

    def sb(name, shape, dt):
        return nc.alloc_sbuf_tensor(name, shape, dt)

    cf = sb("cf", [128, NC], F32)
    identb = sb("identb", [128, 128], BF16)
    onesb = sb("onesb", [128, 128], BF16)
    wgT = sb("wgT", [128, nl * 2 * 4, 128], BF16)
    brow = sb("brow", [1, nl * 2 * 4 * 128], BF16)
    lnt = sb("lnt", [128, 1024], F32)
    xres = sb("xres", [128, 8, TT], F32)
    xn = sb("xn", [128, 8, TT], BF16)
    sqb = [sb("sqb%d" % i, [128, TT], BF16) for i in range(2)]
    sdt = sb("sdt", [128, TT], F32)
    rstd = sb("rstd", [128, TT], F32)
    U = sb("U", [128, NJ * TT], BF16)
    gbuf = U[:, :].rearrange("p (j t) -> p j t", j=NJ)
    qT = U[:, 0:4 * TT].rearrange("p (h t) -> p h t", h=4)
    kT = U[:, 4 * TT:8 * TT].rearrange("p (h t) -> p h t", h=4)
    kd_tok = U[:, 8 * TT:12 * TT].rearrange("p (b c) -> p b c", b=4)
    v_tok = U[:, 12 * TT:16 * TT].rearrange("p (b c) -> p b c", b=4)
    vn_tok = U[:, 16 * TT:20 * TT].rearrange("p (b c) -> p b c", b=4)
    merged = U[:, 0:8 * TT].rearrange("p (m t) -> p m t", m=8)
    og = sb("og", [128, 4, TT], BF16)
    gated = sb("gated", [128, 4, TT], BF16)
    ftmp = [sb("ftmp%d" % i, [128, TT], F32) for i in range(4)]
    rt = [sb("rt%d" % i, [128, 4, 256], F32) for i in range(2)]
    qb_tok = [sb("qbt%d" % i, [128, TT], BF16) for i in range(4)]
    par = [sb("par%d" % i, [128, TT], F32) for i in range(4)]
    vnf = [sb("vnf%d" % i, [128, TT], F32) for i in range(2)]
    st6 = sb("st6", [128, 6], F32)
    mv = sb("mv", [128, 4], F32)
    ppb = [sb("ppb%d" % i, [128, 128], BF16) for i in range(4)]
    S_f = sb("S_f", [128, nl * 4, 128], F32)
    S_b = sb("S_b", [128, nl * 4, 128], BF16)
    osb = sb("osb", [128, TT], F32)
    ocb = sb("ocb", [128, TT], F32)
    onb = sb("onb", [128, TT], F32)
    xst = [sb("xst%d" % i, [128, D], F32) for i in range(2)]
    yst = [sb("yst%d" % i, [128, D], F32) for i in range(2)]
    cs_sb = [sb("cs%d" % i, [128, 2 * 4 * 64], F32) for i in range(2)]
    ring = [sb("ring%d" % i, [128, SLOT_E], BF16) for i in range(NS_RING)]
    wnat = sb("wnat", [128, 128], F32)
    wnm = sb("wnm", [128, 128], F32)
    wrep = sb("wrep", [128, 16], F32)
    browf = sb("browf", [1, 128], F32)
    if has_sample:
        s0f = [sb("s0f%d" % i, [128, 4, 128], F32) for i in range(2)]
        s0b = [sb("s0b%d" % i, [128, 4, 128], BF16) for i in range(2)]
        snew = [sb("snew%d" % i, [128, 4, 128], F32) for i in range(2)]
        kdm = [sb("kdm%d" % i, [128, TT], BF16) for i in range(2)]
    ps = [nc.alloc_psum_tensor("ps%d" % i, [128, 512], F32) for i in range(8)]

    sm = {}

    def dsem(name):
        if name not in sm:
            sm[name] = P.dma_sem("d_" + name)
        return sm[name]

    state = {"bank": 0, "ft": 0, "wpos": 0, "wiss": 0, "pp": 0, "q7": 0, "q6": 0}

    def bank():
        b = ps[state["bank"] % 6]
        state["bank"] += 1
        return b

    def ft():
        t = ftmp[state["ft"] % 4]
        state["ft"] += 1
        return t

    def quarter(which):
        k = state[which] % 4
        state[which] += 1
        return ps[7 if which == "q7" else 6][:, k * 128:(k + 1) * 128]

    ident = cf[:, C_ID:C_ID + 128]

    def vcol(i):
        return cf[:, C_VEC + i:C_VEC + i + 1]

    npc = len(_PIECES)
    wseq = []
    for t in range(nt):
        for l in range(nl):
            for pi in range(npc):
                wseq.append((l, pi))

    def wissue(upto):
        while state["wiss"] <= min(upto, len(wseq) - 1):
            q = state["wiss"]
            l, pi = wseq[q]
            E = _PIECE_E[pi]
            off = _PIECE_OFF[pi]
            slot = q % NS_RING
            gi = _GRP_OF[pi]
            lo = off - _GRPS[gi][0]
            P.dma(ring[slot][:, 0:E], wbf[l][gi][:, lo:lo + E], dsem("ring%d" % slot))
            state["wiss"] += 1

    def wnext(kind, held=0):
        p = state["wpos"]
        state["wiss"] = max(state["wiss"], p)
        l, pi = wseq[p]
        assert _PIECES[pi][0] == kind, (_PIECES[pi][0], kind)
        wissue(p + NS_RING - 1 - held)
        state["wpos"] += 1
        return ring[p % NS_RING]

    if do_cast:
        for l in range(nl):
            for gi, (g0, gn) in enumerate(_GRPS):
                c0 = 0
                while c0 < gn:
                    c1 = min(gn, c0 + 8192)
                    P.dma(wbf[l][gi][:, c0:c1], wf32[l][:, g0 + c0:g0 + c1], dsem("cast%d_%d" % (l, gi)), q="pool")
                    c0 = c1
    P.dma(cf[:, :], cst, dsem("cf"))
    P.copy("act", identb[:, :], ident)
    P.memset("dve", onesb[:, :], 1.0)
    P.memset("dve", S_f[:, :, :], 0.0)
    P.memset("dve", S_b[:, :, :], 0.0)
    for l in range(nl):
        for g in range(4):
            for kind in range(2):
                if (kind == 1 and not has_sample) or "gate" not in stages:
                    continue
                wi = (l * 2 + kind) * 4 + g
                if kind == 0:
                    P.dma(wnat[:, :], gws[l, g, :, :], dsem("wnat"))
                    P.tt("dve", wnm[:, :], wnat[:, :], cf[:, C_MPN:C_MPN + 128], ALU.mult)
                    P.dma(browf[0:1, :], gbs[l, g:g + 1, :], dsem("browf"))
                else:
                    base = (l * 4 + g) * 128 * 128
                    P.dma(wrep[:, :], bass.AP(gws.tensor, base, [[0, 8], [128, 16], [1, 16]]), dsem("wrep"))
                    P.tt("dve", wnm[:, :].rearrange("p (b j) -> p b j", b=8),
                         wrep[:, :].unsqueeze(1).broadcast_to([128, 8, 16]),
                         cf[:, C_MSN:C_MSN + 128].rearrange("p (b j) -> p b j", b=8), ALU.mult)
                    P.dma(browf[0:1, :].rearrange("p (b j) -> p b j", b=8),
                          bass.AP(gbs.tensor, (l * 4 + g) * 128, [[0, 1], [0, 8], [1, 16]]), dsem("browf"))
                q = quarter("q7")
                P.tr(q, wnm[:, :], ident)
                P.copy("act", wgT[:, wi, :], q)
                P.copy("act", brow[0:1, wi * 128:(wi + 1) * 128], browf[0:1, :])

    def rmsnorm(wbase, out_bf=True):
        st = ps[6]
        for c in range(8):
            s = sqb[c % 2]
            P.act(s[:, :], xres[:, c, :], AF.Square)
            P.mm(st[:, :], onesb[:, :], s[:, :], start=(c == 0), stop=(c == 7))
        P.act(sdt[:, :], st[:, :], AF.Sqrt, bias=EPS, scale=1.0 / D)
        P.recip(rstd[:, :], sdt[:, :])
        for c in range(8):
            dst = xn[:, c, :] if out_bf else xres[:, c, :]
            P.stt("dve", dst, xres[:, c, :], vcol(wbase + c), rstd[:, :], ALU.mult, ALU.mult)

    def ffn(wbase):
        rmsnorm(wbase)
        for pc in range(11):
            W = wnext("gu")[:, :].rearrange("p (k f c) -> p k f c", k=8, f=4)
            if pc == 0:
                fb = [bank() for _ in range(4)]
                for kc in range(8):
                    for f in range(4):
                        P.mm(fb[f][:, :], W[:, kc, f, :], xn[:, kc, :], start=(kc == 0), stop=(kc == 7))
            for jj in range(2):
                j = 2 * pc + jj
                if pc == 0:
                    pa, pb = fb[jj * 2], fb[jj * 2 + 1]
                else:
                    pa = bank()
                    pb = bank()
                    for kc in range(8):
                        P.mm(pa[:, :], W[:, kc, jj * 2, :], xn[:, kc, :], start=(kc == 0), stop=(kc == 7))
                    for kc in range(8):
                        P.mm(pb[:, :], W[:, kc, jj * 2 + 1, :], xn[:, kc, :], start=(kc == 0), stop=(kc == 7))
                sl = ft()
                P.act(sl[:, :], pa[:, :], AF.Silu)
                P.tt("dve", gbuf[:, j, :], sl[:, :], pb[:, :], ALU.mult)
        for m in range(8):
            W = wnext("down")[:, 0:NJ * 128].rearrange("p (j c) -> p j c", j=NJ)
            acc = bank()
            for j in range(NJ):
                P.mm(acc[:, :], W[:, j, :], gbuf[:, j, :], start=(j == 0), stop=(j == NJ - 1))
            P.stt("dve", xres[:, m, :], acc[:, :], 0.5, xres[:, m, :], ALU.mult, ALU.add)

    def rotary(src, cs, blk, dst, slot):
        sv = src.rearrange("p (h t f) -> p h t f", h=4, t=2)
        dv = dst.rearrange("p (h t f) -> p h t f", h=4, t=2)
        x1 = sv[:, :, 0, :]
        x2 = sv[:, :, 1, :]
        csv = cs[:, :].rearrange("p (a b f) -> p a b f", a=2, b=4)
        cos = csv[:, 0, blk, :].unsqueeze(1).broadcast_to([128, 4, 64])
        sin = csv[:, 1, blk, :].unsqueeze(1).broadcast_to([128, 4, 64])
        r = rt[slot][:, :, :].rearrange("p a (h f) -> p a h f", h=4)
        P.tt("dve", r[:, 0], x1, cos, ALU.mult)
        P.tt("dve", r[:, 1], x2, sin, ALU.mult)
        P.tt("dve", dv[:, :, 0, :], r[:, 0], r[:, 1], ALU.subtract)
        P.tt("pool", r[:, 2], x1, sin, ALU.mult)
        P.tt("pool", r[:, 3], x2, cos, ALU.mult)
        P.tt("pool", dv[:, :, 1, :], r[:, 2], r[:, 3], ALU.add)

    def mixer(t, l, kind, wbase, cs, si):
        rmsnorm(wbase)
        P.dma(lnt[:, :], bass.AP(lnwb.tensor, l * 1024, [[0, 128], [1, 1024]]), dsem("lnt"))
        gb = gblk(kind)
        kofs = kind * 4
        pend = []

        def flush(keep):
            while len(pend) > keep:
                nm_, blk_, xb_ = pend.pop(0)
                tk = slice(blk_ * 128, (blk_ + 1) * 128)
                pT = bank()
                pTb = pT[:, :].bitcast(BF16)
                for h in range(4):
                    P.tr(pTb[:, h * 128:(h + 1) * 128], xb_[:, h * 128:(h + 1) * 128], identb[:, :], inc=(h == 3))
                P.copy("act", (qT if nm_ == "q" else kT)[:, :, tk], pTb[:, 0:512].rearrange("p (h t) -> p h t", h=4))

        cnt = 0
        for nm in ("q", "k", "vr", "vm"):
            W = wnext(nm)[:, :].rearrange("p (k c) -> p k c", k=8)
            pps = None
            if nm == "q":
                pps = [bank() for _ in range(4)]
                for kc in range(8):
                    for blk in range(4):
                        P.mm(pps[blk][:, :], xn[:, kc, blk * 128:(blk + 1) * 128], W[:, kc, :], start=(kc == 0), stop=(kc == 7))
            for blk in range(4):
                tok = slice(blk * 128, (blk + 1) * 128)
                if pps is not None:
                    pp = pps[blk]
                else:
                    pp = bank()
                    for kc in range(8):
                        P.mm(pp[:, :], xn[:, kc, tok], W[:, kc, :], start=(kc == 0), stop=(kc == 7))
                if nm in ("q", "k"):
                    f = par[cnt % 4]
                    xb = qb_tok[cnt % 4]
                    P.copy("act", f[:, :], pp[:, :])
                    rotary(f[:, :], cs, blk, xb[:, :], cnt % 2)
                    cnt += 1
                    if nm == "k":
                        P.tt("dve", kd_tok[:, blk, :].rearrange("p (h e) -> p h e", h=4),
                             xb[:, :].rearrange("p (h e) -> p h e", h=4),
                             cf[:, C_KDEC + kofs:C_KDEC + kofs + 4].unsqueeze(2).broadcast_to([128, 4, 128]), ALU.mult)
                    pend.append((nm, blk, xb))
                    flush(2)
                elif nm == "vr":
                    P.copy("act", v_tok[:, blk, :], pp[:, :])
                    flush(max(0, len(pend) - 1))
                else:
                    P.rec("dve", lambda e, pp=pp: e.bn_stats(st6[:, :], pp[:, :]), [pp[:, :]], [st6[:, :]])
                    P.rec("dve", lambda e: e.bn_aggr(mv[:, 0:2], st6[:, :]), [st6[:, :]], [mv[:, 0:2]])
                    P.act(mv[:, 2:3], mv[:, 1:2], AF.Sqrt, bias=EPS, scale=1.0)
                    P.recip(mv[:, 3:4], mv[:, 2:3])
                    z = ft()
                    P.ts("dve", z[:, :], pp[:, :], mv[:, 0:1], mv[:, 3:4], ALU.subtract, ALU.mult)
                    vf = vnf[blk % 2]
                    P.tt("pool", z[:, :], z[:, :], lnt[:, 0:512], ALU.mult)
                    P.tt("pool", vf[:, :], z[:, :], lnt[:, 512:1024], ALU.add)
                    P.copy("act", vn_tok[:, blk, :], vf[:, :])
                    if kind == 1:
                        P.dma(gvo[si, l, blk * 128:(blk + 1) * 128, :], vf[:, :], dsem("vnf%d" % (blk % 2)))
        flush(0)
        oT = [ps[h] for h in range(4)]
        mask = cf[:, (C_MP if kind == 0 else C_MS):(C_MP if kind == 0 else C_MS) + 128]
        for blk in range(4):
            tok = slice(blk * 128, (blk + 1) * 128)
            for h in range(4):
                hs = slice(h * 128, (h + 1) * 128)
                sc = ps[6 + (state["q6"] % 2)][:, 0:128]
                state["q6"] += 1
                P.mm(sc, kT[:, h, tok], qT[:, h, tok])
                pb_ = ppb[state["pp"] % 4]
                state["pp"] += 1
                P.stt("dve", pb_[:, :], sc, cf[:, C_GNEG + kofs + h:C_GNEG + kofs + h + 1], mask, ALU.mult, ALU.mult)
                P.mm(oT[h][:, tok], v_tok[:, blk, hs], pb_[:, :], start=True, stop=False, inc=False)
                if kind == 0:
                    P.mm(oT[h][:, tok], S_b[:, l * 4 + h, :], qT[:, h, tok], start=False, stop=True)
                    kv = ps[4 + (h % 2)][:, 0:128]
                    P.mm(kv, kd_tok[:, blk, hs], v_tok[:, blk, hs])
                    P.stt("dve", S_f[:, l * 4 + h, :], S_f[:, l * 4 + h, :], gb[h], kv, ALU.mult, ALU.add)
                    P.copy("act", S_b[:, l * 4 + h, :], S_f[:, l * 4 + h, :])
            if kind == 1:
                for s in range(8):
                    b = blk * 8 + s
                    sl = b % 2
                    P.dma(s0f[sl][:, :, :], sret[l, b].rearrange("h d e -> d h e"), dsem("s0f%d" % sl))
                    P.copy("act", s0b[sl][:, :, :], s0f[sl][:, :, :])
                    P.ts("dve", kdm[sl][:, :], kd_tok[:, blk, :], cf[:, C_SEQM + s:C_SEQM + s + 1], None, ALU.mult)
                    cols = slice(b * 16, (b + 1) * 16)
                    if "xA" in stages:
                        P.copy("dve", snew[sl][:, :, :], s0f[sl][:, :, :])
                    else:
                        for h in range(4):
                            hs = slice(h * 128, (h + 1) * 128)
                            if "xB" not in stages:
                                P.mm(oT[h][:, cols], s0b[sl][:, h, :], qT[:, h, cols], start=False, stop=(s == 7), inc=(s == 7))
                            kv = ps[4 + (h % 2)][:, 0:128]
                            P.mm(kv, kdm[sl][:, hs], v_tok[:, blk, hs])
                            P.stt("dve", snew[sl][:, h, :], s0f[sl][:, h, :], gb[h], kv, ALU.mult, ALU.add)
                    P.dma(sso[si, l, b].rearrange("h d e -> d h e"), snew[sl][:, :, :], dsem("snew%d" % sl))
        if kind == 0 and t >= npt - 2:
            P.dma(spo[t - (npt - 2), l].rearrange("h d e -> d h e"), S_f[:, l * 4:(l + 1) * 4, :], dsem("spo"))
        if "mD" not in stages:
            state["wpos"] += 12
            return
        Wg = wnext("gr")[:, :].rearrange("p (k c) -> p k c", k=8)
        Wu = wnext("u", held=1)[:, :].rearrange("p (k c) -> p k c", k=8)
        gpos0 = C_GPOS + kind * 512
        gnbase = nl * 24 + 8 + l * 4
        gate_ps = {}

        def gate_mm(g):
            gs = slice(g * 128, (g + 1) * 128)
            wi = (l * 2 + kind) * 4 + g
            sT = ps[4]
            for blk in range(4):
                tok = slice(blk * 128, (blk + 1) * 128)
                P.mm(sT[:, tok], vn_tok[:, blk, gs], wgT[:, wi, :], start=True, stop=False, inc=False)
                P.mm(sT[:, tok], onesb[0:1, :], brow[0:1, wi * 128:(wi + 1) * 128], start=False, stop=True,
                     inc=(blk == 3))
            uT = ps[5]
            for kc in range(8):
                P.mm(uT[:, :], Wu[:, kc, gs], xn[:, kc, :], start=(kc == 0), stop=(kc == 7))
            gate_ps[g] = (sT, uT)

        def gate_ev(g):
            sT, uT = gate_ps[g]
            ssb = ft()
            P.copy("act", ssb[:, :], sT[:, :])
            P.tt("dve", gated[:, g, :], ssb[:, :], uT[:, :], ALU.mult)

        def d1(h):
            gp = cf[:, gpos0 + h * 128:gpos0 + (h + 1) * 128].unsqueeze(1).broadcast_to([128, 4, 128])
            P.tt("dve", osb[:, :].rearrange("p (b i) -> p b i", b=4),
                 oT[h][:, :].rearrange("p (b i) -> p b i", b=4), gp, ALU.mult)
            P.copy("act", sqb[0][:, :], osb[:, :])
            P.mm(ps[6][:, :], onesb[:, :], sqb[0][:, :])

        def d2(h):
            P.stt("dve", ocb[:, :], ps[6][:, :], -1.0 / 128, osb[:, :], ALU.mult, ALU.add)
            P.act(sqb[1][:, :], ocb[:, :], AF.Square)
            P.mm(ps[7][:, :], onesb[:, :], sqb[1][:, :])

        def d3(h):
            hs = slice(h * 128, (h + 1) * 128)
            pg = ps[h]
            for kc in range(8):
                P.mm(pg[:, :], Wg[:, kc, hs], xn[:, kc, :], start=(kc == 0), stop=(kc == 7))
            P.act(sdt[:, :], ps[7][:, :], AF.Sqrt, bias=EPS, scale=1.0 / 128)
            P.recip(rstd[:, :], sdt[:, :])
            P.stt("dve", onb[:, :], ocb[:, :], vcol(gnbase + h), rstd[:, :], ALU.mult, ALU.mult)
            sg = ft()
            P.act(sg[:, :], pg[:, :], AF.Silu)
            P.tt("pool", og[:, h, :], onb[:, :], sg[:, :], ALU.mult)

        d1(0)
        for h in range(4):
            gate_mm(h)
            d2(h)
            gate_ev(h)
            d3(h)
            if h < 3:
                d1(h + 1)
        for m in range(8):
            W = wnext("merge")[:, 0:24 * 128].rearrange("p (b c) -> p b c", b=24)
            pa = bank()
            pm = bank()
            py = bank()
            pz = bank()
            for kc in range(8):
                P.mm(pa[:, :], W[:, kc, :], xn[:, kc, :], start=(kc == 0), stop=(kc == 7))
            for kc in range(8):
                P.mm(pm[:, :], W[:, 8 + kc, :], xn[:, kc, :], start=(kc == 0), stop=(kc == 7))
            for kc in range(4):
                P.mm(py[:, :], W[:, 16 + kc, :], og[:, kc, :], start=(kc == 0), stop=(kc == 3))
            for kc in range(4):
                P.mm(pz[:, :], W[:, 20 + kc, :], gated[:, kc, :], start=(kc == 0), stop=(kc == 3))
            sr = ft()
            sm_ = ft()
            P.act(sr[:, :], pa[:, :], AF.Sigmoid)
            P.act(sm_[:, :], pm[:, :], AF.Sigmoid)
            P.tt("dve", sr[:, :], sr[:, :], py[:, :], ALU.mult)
            P.tt("dve", sm_[:, :], sm_[:, :], pz[:, :], ALU.mult)
            P.tt("pool", merged[:, m, :], sr[:, :], sm_[:, :], ALU.add)
        for hf in range(2):
            W = wnext("wout")[:, :].rearrange("p (k c) -> p k c", k=8)
            for mm_ in range(4):
                m = hf * 4 + mm_
                acc = bank()
                for kc in range(8):
                    P.mm(acc[:, :], W[:, kc, mm_ * 128:(mm_ + 1) * 128], merged[:, kc, :], start=(kc == 0), stop=(kc == 7))
                P.tt("dve", xres[:, m, :], acc[:, :], xres[:, m, :], ALU.add)

    cc_sem = None
    if pipelined:
        cc_sem = Sem("cc", 1)
        P.dsems.append(cc_sem)
        P.memset("dve", xst[0][:, :], 0.0)
        for k in range(4):
            P.dma(gath[0:128, k * D:(k + 1) * D], xst[0][:, :], dsem("gz"))

    def store_tile(dst_rows):
        for blk in range(4):
            ys = yst[blk % 2]
            for hf in range(2):
                pT = bank()
                for cc in range(4):
                    c = hf * 4 + cc
                    P.tr(pT[:, cc * 128:(cc + 1) * 128], xres[:, c, blk * 128:(blk + 1) * 128], ident, inc=(cc == 3))
                P.copy("act" if hf == 0 else "dve", ys[:, hf * 512:(hf + 1) * 512], pT[:, :])
            P.dma(dst_rows(blk), ys[:, :], dsem("yst%d" % (blk % 2)))

    for t in range(nt):
        kind = 0 if t < npt else 1
        si = max(0, t - npt)
        cs = cs_sb[t % 2]
        P.dma(cs[:, :], cstab[t], dsem("cs%d" % (t % 2)))
        xflat = xres[:, :, :].rearrange("p c t -> p (c t)")
        if pipelined:
            P.dma(xflat, gath[0:128, :], dsem("recv"))
        for blk in range(4):
            xs = xst[blk % 2]
            P.dma(xs[:, :], xin[t * TT + blk * 128:t * TT + (blk + 1) * 128, :], dsem("xst%d" % (blk % 2)))
            for hf in range(2):
                pT = bank()
                for cc in range(4):
                    c = hf * 4 + cc
                    P.tr(pT[:, cc * 128:(cc + 1) * 128], xs[:, c * 128:(c + 1) * 128], ident, inc=(cc == 3))
                dst = xres[:, hf * 4:(hf + 1) * 4, blk * 128:(blk + 1) * 128]
                if pipelined:
                    P.stt("dve", dst, dst, cf[:, C_MA + 1:C_MA + 2], pT[:, :].rearrange("p (c t) -> p c t", c=4),
                          ALU.mult, ALU.add)
                else:
                    P.copy("act" if hf == 0 else "dve", dst, pT[:, :].rearrange("p (c t) -> p c t", c=4))
        for l in range(nl):
            if "ffn1" in stages:
                ffn(l * 24)
            else:
                state["wpos"] += 19
            if "mix" in stages:
                mixer(t, l, kind, l * 24 + 8, cs, si)
            else:
                state["wpos"] += 16
            if "ffn2" in stages:
                ffn(l * 24 + 16)
            else:
                state["wpos"] += 19
        if pipelined and t < nt - 1:
            P.dma(send, xres[:, :, :].rearrange("p c t -> p (c t)"), dsem("sendfm"))
            P.rec("pool", lambda e: e.collective_compute("AllGather", ALU.bypass, replica_groups=groups,
                                                         ins=[send], outs=[gath]),
                  [send], [gath], csem=cc_sem)
        rmsnorm(nl * 24, out_bf=False)
        store_tile(lambda blk, t=t: yout[t * TT + blk * 128:t * TT + (blk + 1) * 128, :])
    assert state["wpos"] == len(wseq)
    P.emit()
    return nc, P


def kernel(x_prompt, x_sample, state_ret, ffn1_norm, ffn1_w_gu, ffn1_w_down, mix_norm, w_in, ret_gn_w,
           gmlp_ln_w, gmlp_ln_b, gmlp_ws, gmlp_bs, w_br_ret, w_br_mlp, w_out, ffn2_norm, ffn2_w_gu,
           ffn2_w_down, final_norm):
    w = dict(ffn1_norm=ffn1_norm, ffn1_w_gu=ffn1_w_gu, ffn1_w_down=ffn1_w_down, mix_norm=mix_norm, w_in=w_in,
             ret_gn_w=ret_gn_w, w_br_ret=w_br_ret, w_br_mlp=w_br_mlp, w_out=w_out, ffn2_norm=ffn2_norm,
             ffn2_w_gu=ffn2_w_gu, ffn2_w_down=ffn2_w_down, final_norm=final_norm)
    w = {k: np.asarray(v, dtype=np.float32) for k, v in w.items()}
    x_prompt = np.asarray(x_prompt, np.float32)
    x_sample = np.asarray(x_sample, np.float32)
    state_ret = np.ascontiguousarray(np.asarray(state_ret, np.float32))
    B, L, _ = x_prompt.shape
    nl = state_ret.shape[0]
    nb_s, ls = x_sample.shape[0], x_sample.shape[1]
    past = 4096
    npr = L // TT
    npt = npr + 1
    ns = 2
    nlc = nl // 2
    n_cores = 8
    groups = [[2 * p, 2 * p + 1] for p in range(n_cores // 2)]
    nc, _ = build_program(npt, nlc, ns, pipelined=True, groups=groups)
    xs_flat = x_sample.reshape(nb_s * ls, D)
    zt = np.zeros((TT, D), np.float32)
    pos_p = np.arange(L, dtype=np.float32).reshape(npr, TT)
    pos_s = (past + (np.arange(TT) % ls)).astype(np.float32)[None, :]
    pz = np.zeros((1, TT), np.float32)
    lnw = np.asarray(gmlp_ln_w, np.float32)
    lnb = np.asarray(gmlp_ln_b, np.float32)
    gws_all = np.asarray(gmlp_ws, np.float32)
    gbs_all = np.asarray(gmlp_bs, np.float32)
    role_in = []
    for role in range(2):
        lsl = slice(role * nlc, (role + 1) * nlc)
        wc = {k: (v if k == "final_norm" else v[lsl]) for k, v in w.items()}
        pos = np.concatenate([pos_p, pz, pos_s, pz] if role == 0 else [pz, pos_p, pz, pos_s], axis=0)
        role_in.append({
            "wf32": np.stack([pack_weights(wc, l) for l in range(nlc)]),
            "cst": build_consts(nlc, wc, m_a=1.0 if role == 0 else 0.0),
            "cstab": build_cstab(pos).reshape(npt + ns, 128, 512),
            "lnwb": np.ascontiguousarray(np.concatenate([lnw[lsl], lnb[lsl]], axis=1)),
            "gws": np.ascontiguousarray(gws_all[lsl]),
            "gbs": np.ascontiguousarray(gbs_all[lsl]),
            "sret": np.ascontiguousarray(state_ret[lsl]),
        })
    xin_b = np.zeros(((npt + ns) * TT, D), np.float32)
    in_maps = []
    for c in range(n_cores):
        p, role = c // 2, c % 2
        m = dict(role_in[role])
        if role == 0:
            m["xin"] = np.ascontiguousarray(np.concatenate([x_prompt[p % B], zt, xs_flat, zt], axis=0))
        else:
            m["xin"] = xin_b
        in_maps.append(m)
    res = run_bass_kernel_spmd(nc, in_maps, core_ids=list(range(n_cores)))
    r = res.results
    y_prompt = np.stack([r[2 * b + 1]["yout"][TT:TT + L] for b in range(B)]).astype(np.float32)
    y_sample = r[1]["yout"][(npt + 1) * TT:(npt + 2) * TT].reshape(nb_s, ls, D).astype(np.float32)
    sp = np.stack([np.concatenate([r[2 * b]["spo"][0], r[2 * b + 1]["spo"][1]], axis=0) for b in range(B)],
                  axis=1).astype(np.float32)
    ss = np.concatenate([r[0]["sso"][0], r[1]["sso"][1]], axis=0).astype(np.float32)
    gv = np.concatenate([r[0]["gvo"][0], r[1]["gvo"][1]], axis=0).reshape(nl, nb_s, ls, 512).astype(np.float32)
    return (y_prompt, y_sample, sp, ss, gv)
```

```python
import contextlib
import numpy as np
import concourse.bass as bass
import concourse.mybir as mybir
from concourse.bass_utils import run_bass_kernel_spmd

F32 = mybir.dt.float32
BF16 = mybir.dt.bfloat16
ALU = mybir.AluOpType
AF = mybir.ActivationFunctionType
_ESZ = {F32: 4, BF16: 2}

D = 1024
DFF = 2816
NJ = 22
NH = 4
TT = 512
EPS = 1e-6
WTOT = 192512
NS_RING = 5
SLOT_E = 4096


class Sem:
    __slots__ = ("name", "step", "h", "total")

    def __init__(self, name, step):
        self.name = name
        self.step = step
        self.h = None
        self.total = 0


class Op:
    __slots__ = ("q", "csem", "fn", "deps", "inc", "cnt", "sig", "idx")


def _region(ap):
    t = ap.tensor
    name = t.name
    sp = str(ap.space).upper()
    if "DRAM" in sp or "HBM" in sp:
        return name, 0, 1 << 60
    if "PSUM" in sp:
        return name.split("_bitcast")[0], 0, 2048
    F = 1
    for s in list(t.shape)[1:]:
        F *= int(s)
    esz = _ESZ.get(t.dtype, 4)
    off = int(ap.offset) % F
    ext = 0
    for (st, cn) in list(ap.ap)[1:]:
        ext += (int(cn) - 1) * abs(int(st))
    return name, off * esz, (off + ext + 1) * esz


class Prog:
    QUEUES = ("pe", "act", "dve", "pool", "sp")

    def __init__(self, nc):
        self.nc = nc
        self.ops = []
        self.qops = {q: [] for q in self.QUEUES}
        self.acc = {}
        self.esem = {q: Sem("s_" + q, 1) for q in ("pe", "act", "dve", "pool")}
        self.dsems = []
        self.pe_pending = []

    def dma_sem(self, name):
        s = Sem(name, 16)
        self.dsems.append(s)
        return s

    def rec(self, q, fn, reads, writes, csem=None, inc=True):
        op = Op()
        op.q = q
        op.fn = fn
        op.csem = csem if csem is not None else self.esem[q]
        op.inc = inc
        op.cnt = None
        op.sig = None
        idx = len(self.ops)
        op.idx = idx
        deps = set()
        csid = id(op.csem)
        for ap, is_w in [(r, False) for r in reads] + [(w, True) for w in writes]:
            name, lo, hi = _region(ap)
            d = self.acc.get(name)
            if d is None:
                d = {}
                self.acc[name] = d
            dead = []
            for key, j in d.items():
                l2, h2, cs2, w2 = key
                if j == idx:
                    continue
                if l2 < hi and lo < h2 and (is_w or w2):
                    deps.add(j)
                    if is_w and lo <= l2 and h2 <= hi:
                        dead.append(key)
            for k in dead:
                del d[k]
            d[(lo, hi, csid, is_w)] = idx
        real = set()
        for j in deps:
            o = self.ops[j]
            if q == "pe" and o.q == "pe":
                continue
            if not o.inc:
                if o.sig is None:
                    o.inc = True
                    k = self.pe_pending.index(o)
                    for p in self.pe_pending[: k + 1]:
                        p.sig = o
                    del self.pe_pending[: k + 1]
                real.add(o.sig.idx)
            else:
                real.add(j)
        op.deps = real
        self.ops.append(op)
        self.qops[q].append(op)
        if q == "pe":
            if inc:
                for p in self.pe_pending:
                    p.sig = op
                self.pe_pending = []
                op.sig = op
            else:
                self.pe_pending.append(op)
        else:
            op.sig = op
        return op

    def mm(self, out, lhsT, rhs, start=True, stop=True, inc=None):
        if inc is None:
            inc = stop
        return self.rec("pe", lambda e: e.matmul(out, lhsT, rhs, start=start, stop=stop),
                        [lhsT, rhs], [out], inc=inc)

    def tr(self, out, in_, ident, inc=True):
        return self.rec("pe", lambda e: e.transpose(out, in_, ident), [in_, ident], [out], inc=inc)

    def act(self, out, in_, func, bias=0.0, scale=1.0):
        rd = [in_]
        if not isinstance(bias, (int, float)):
            rd.append(bias)
        if not isinstance(scale, (int, float)):
            rd.append(scale)
        return self.rec("act", lambda e: e.activation(out, in_, func, bias=bias, scale=scale), rd, [out])

    def tt(self, q, out, in0, in1, op):
        return self.rec(q, lambda e: e.tensor_tensor(out, in0, in1, op), [in0, in1], [out])

    def ts(self, q, out, in0, s1, s2, op0, op1=ALU.bypass):
        rd = [in0]
        for s in (s1, s2):
            if s is not None and not isinstance(s, (int, float)):
                rd.append(s)
        if s2 is None:
            return self.rec(q, lambda e: e.tensor_scalar(out, in0, s1, None, op0), rd, [out])
        return self.rec(q, lambda e: e.tensor_scalar(out, in0, s1, s2, op0, op1), rd, [out])

    def stt(self, q, out, in0, scalar, in1, op0, op1):
        rd = [in0, in1]
        if not isinstance(scalar, (int, float)):
            rd.append(scalar)
        return self.rec(q, lambda e: e.scalar_tensor_tensor(out, in0, scalar, in1, op0, op1), rd, [out])

    def copy(self, q, out, in_):
        if q == "act":
            return self.rec(q, lambda e: e.activation(out, in_, AF.Copy), [in_], [out])
        return self.rec(q, lambda e: e.tensor_copy(out, in_), [in_], [out])

    def memset(self, q, out, val):
        return self.rec(q, lambda e: e.memset(out, val), [], [out])

    def recip(self, out, in_):
        return self.rec("dve", lambda e: e.reciprocal(out, in_), [in_], [out])

    def dma(self, out, in_, sem, q="sp"):
        return self.rec(q, lambda e: e.dma_start(out=out, in_=in_), [in_], [out], csem=sem)

    def emit(self):
        nc = self.nc
        for op in self.ops:
            if op.inc:
                op.csem.total += op.csem.step
                op.cnt = op.csem.total
        allsems = list(self.esem.values()) + self.dsems
        with contextlib.ExitStack() as st:
            for s in allsems:
                s.h = st.enter_context(nc.semaphore(s.name))
            block = st.enter_context(nc.Block())
            ops = self.ops

            def run(q, e):
                seen = {}
                for op in self.qops[q]:
                    need = {}
                    for j in op.deps:
                        o = ops[j]
                        cs = o.csem
                        c = o.cnt
                        if need.get(cs, 0) < c:
                            need[cs] = c
                    for cs, c in need.items():
                        if seen.get(cs, 0) >= c:
                            continue
                        seen[cs] = c
                        e.wait_ge(cs.h, c)
                    ins = op.fn(e)
                    if op.inc:
                        ins.then_inc(op.csem.h, op.csem.step)
                if q == "sp":
                    for s in self.dsems:
                        if s.total > 0:
                            e.wait_ge(s.h, s.total)
                    for s in self.esem.values():
                        if s.total > 0:
                            e.wait_ge(s.h, s.total)

            mp = {"pe": block.tensor, "act": block.scalar, "dve": block.vector,
                  "pool": block.gpsimd, "sp": block.sync}
            for q in self.QUEUES:
                if q == "sp" or self.qops[q]:
                    mp[q](lambda e, q=q: run(q, e))


def layer_pieces():
    pcs = []

    def ffn(tag):
        for pc in range(11):
            bl = []
            for kc in range(8):
                for jj in range(2):
                    j = 2 * pc + jj
                    bl.append((tag + "_w_gu", kc * 128, j * 128, 128))
                    bl.append((tag + "_w_gu", kc * 128, DFF + j * 128, 128))
            pcs.append(("gu", bl))
        for m in range(8):
            pcs.append(("down", [(tag + "_w_down", j * 128, m * 128, 128) for j in range(NJ)]))

    ffn("ffn1")
    for nm, c0 in (("q", 0), ("k", 512), ("vr", 1024), ("vm", 2560), ("gr", 1536), ("u", 2048)):
        pcs.append((nm, [("w_in", kc * 128, c0, 512) for kc in range(8)]))
    for m in range(8):
        bl = [("w_in", kc * 128, 3072 + m * 128, 128) for kc in range(8)]
        bl += [("w_in", kc * 128, 4096 + m * 128, 128) for kc in range(8)]
        bl += [("w_br_ret", kc * 128, m * 128, 128) for kc in range(4)]
        bl += [("w_br_mlp", kc * 128, m * 128, 128) for kc in range(4)]
        pcs.append(("merge", bl))
    for hf in range(2):
        pcs.append(("wout", [("w_out", kc * 128, hf * 512, 512) for kc in range(8)]))
    ffn("ffn2")
    return pcs


_PIECES = layer_pieces()
_PIECE_E = [sum(b[3] for b in bl) for (_, bl) in _PIECES]
_PIECE_OFF = [int(x) for x in np.concatenate([[0], np.cumsum(_PIECE_E)[:-1]])]
assert sum(_PIECE_E) == WTOT and max(_PIECE_E) <= SLOT_E
_GRP_OF = []
_GRPS = []
_g0, _gn = 0, 0
for _pi, _e in enumerate(_PIECE_E):
    if False:
        _GRPS.append((_g0, _gn))
        _g0, _gn = _g0 + _gn, 0
    _GRP_OF.append(len(_GRPS))
    _gn += _e
_GRPS.append((_g0, _gn))


def pack_weights(w, l):
    cols = []
    for (_, bl) in _PIECES:
        for (key, r0, c0, n) in bl:
            cols.append(w[key][l][r0:r0 + 128, c0:c0 + n])
    return np.ascontiguousarray(np.concatenate(cols, axis=1), dtype=np.float32)


C_ID = 0
C_MP = 128
C_MS = 256
C_MPN = 384
C_MSN = 512
C_GNEG = 640
C_KDEC = 648
C_SEQM = 656
C_GPOS = 664
C_VEC = 664 + 1024


def build_consts(nl, w, m_a=1.0):
    gam = (1.0 - 2.0 ** (-5.0 - np.arange(NH))).astype(np.float64)
    p = np.arange(128)
    nvec = nl * 3 * 8 + 8 + nl * 4 + 2
    c = np.zeros((128, C_VEC + nvec), np.float32)
    c[:, C_ID:C_ID + 128] = np.eye(128)
    jj, ii = np.meshgrid(p, p, indexing="ij")
    mp = (ii >= jj)
    ms = mp & ((ii // 16) == (jj // 16))
    c[:, C_MP:C_MP + 128] = mp
    c[:, C_MS:C_MS + 128] = ms
    c[:, C_MPN:C_MPN + 128] = mp.T
    c[:, C_MSN:C_MSN + 128] = ms.T
    sc = 128.0 ** -0.5
    for h in range(NH):
        c[:, C_GNEG + h] = sc * gam[h] ** (-(p + 1.0))
        c[:, C_GNEG + 4 + h] = sc * gam[h] ** (-((p % 16) + 1.0))
        c[:, C_KDEC + h] = sc * gam[h] ** (127.0 - p)
        c[:, C_KDEC + 4 + h] = sc * gam[h] ** (15.0 - (p % 16))
        c[:, C_GPOS + h * 128:C_GPOS + (h + 1) * 128] = (gam[h] ** (p + 1.0))[None, :]
        c[:, C_GPOS + 512 + h * 128:C_GPOS + 512 + (h + 1) * 128] = (gam[h] ** ((p % 16) + 1.0))[None, :]
    for s in range(8):
        c[:, C_SEQM + s] = (p // 16) == s
    o = C_VEC
    for l in range(nl):
        for i, key in enumerate(("ffn1_norm", "mix_norm", "ffn2_norm")):
            c[:, o:o + 8] = w[key][l].reshape(8, 128).T
            o += 8
    c[:, o:o + 8] = w["final_norm"].reshape(8, 128).T
    o += 8
    for l in range(nl):
        c[:, o:o + 4] = w["ret_gn_w"][l].reshape(4, 128).T
        o += 4
    c[:, o] = m_a
    c[:, o + 1] = 1.0 - m_a
    return c


def gblk(kind):
    gam = 1.0 - 2.0 ** (-5.0 - np.arange(NH))
    return [float(g ** (128.0 if kind == 0 else 16.0)) for g in gam]


def build_cstab(positions):
    freqs = (np.float32(10000.0) ** (-np.arange(64, dtype=np.float32) / np.float32(64))).astype(np.float32)
    ang = positions.astype(np.float32)[..., None] * freqs[None, None, :]
    cs = np.stack([np.cos(ang), np.sin(ang)], axis=1).astype(np.float32)
    nt = positions.shape[0]
    cs = cs.reshape(nt, 2, 4, 128, 64).transpose(0, 3, 1, 2, 4)
    return np.ascontiguousarray(cs)


def build_program(npt, nl, ns=1, pipelined=False, groups=None, do_cast=True,
                  stages=("gate", "ffn1", "mix", "mC", "mD", "mE", "mF", "ffn2")):
    nc = bass.Bass("TRN2", target_bir_lowering=False)
    P = Prog(nc)
    has_sample = ns > 0
    nt = npt + ns
    ntok = nt * TT
    nvec = nl * 3 * 8 + 8 + nl * 4 + 2
    NC = C_VEC + nvec
    C_MA = C_VEC + nvec - 2

    def din(name, shape, dt=F32):
        return nc.dram_tensor(name, shape, dt, kind="ExternalInput").ap()

    def dout(name, shape):
        return nc.dram_tensor(name, shape, F32, kind="ExternalOutput").ap()

    xin = din("xin", [ntok, D])
    wf32 = din("wf32", [nl, 128, WTOT])
    cst = din("cst", [128, NC])
    cstab = din("cstab", [nt, 128, 2 * 4 * 64])
    lnwb = din("lnwb", [nl, 2 * 512])
    gws = din("gws", [nl, 4, 128, 128])
    gbs = din("gbs", [nl, 4, 128])
    sret = din("sret", [nl, 32, 4, 128, 128])
    yout = dout("yout", [ntok, D])
    spo = dout("spo", [2, nl, 4, 128, 128])
    sso = dout("sso", [max(ns, 1), nl, 32, 4, 128, 128])
    gvo = dout("gvo", [max(ns, 1), nl, 512, 512])
    send = nc.dram_tensor("sendbuf", [TT, D], F32).ap()
    gath = nc.dram_tensor("gathbuf", [2 * TT, D], F32).ap()
    wbf = [[nc.dram_tensor("wbf%d_%d" % (l, gi), [128, gn], BF16).ap() for gi, (g0, gn) in enumerate(_GRPS)]
           for l in range(nl)]

    def sb(name, shape, dt):
        return nc.alloc_sbuf_tensor(name, shape, dt)

    cf = sb("cf", [128, NC], F32)
    identb = sb("identb", [128, 128], BF16)
    onesb = sb("onesb", [128, 128], BF16)
    wgT = sb("wgT", [128, nl * 2 * 4, 128], BF16)
    brow = sb("brow", [1, nl * 2 * 4 * 128], BF16)
    lnt = sb("lnt", [128, 1024], F32)
    xres = sb("xres", [128, 8, TT], F32)
    xn = sb("xn", [128, 8, TT], BF16)
    sqb = [sb("sqb%d" % i, [128, TT], BF16) for i in range(2)]
    sdt = sb("sdt", [128, TT], F32)
    rstd = sb("rstd", [128, TT], F32)
    U = sb("U", [128, NJ * TT], BF16)
    gbuf = U[:, :].rearrange("p (j t) -> p j t", j=NJ)
    qT = U[:, 0:4 * TT].rearrange("p (h t) -> p h t", h=4)
    kT = U[:, 4 * TT:8 * TT].rearrange("p (h t) -> p h t", h=4)
    kd_tok = U[:, 8 * TT:12 * TT].rearrange("p (b c) -> p b c", b=4)
    v_tok = U[:, 12 * TT:16 * TT].rearrange("p (b c) -> p b c", b=4)
    vn_tok = U[:, 16 * TT:20 * TT].rearrange("p (b c) -> p b c", b=4)
    merged = U[:, 0:8 * TT].rearrange("p (m t) -> p m t", m=8)
    og = sb("og", [128, 4, TT], BF16)
    gated = sb("gated", [128, 4, TT], BF16)
    ftmp = [sb("ftmp%d" % i, [128, TT], F32) for i in range(4)]
    rt = [sb("rt%d" % i, [128, 4, 256], F32) for i in range(2)]
    qb_tok = [sb("qbt%d" % i, [128, TT], BF16) for i in range(4)]
    par = [sb("par%d" % i, [128, TT], F32) for i in range(4)]
    vnf = [sb("vnf%d" % i, [128, TT], F32) for i in range(2)]
    st6 = sb("st6", [128, 6], F32)
    mv = sb("mv", [128, 4], F32)
    ppb = [sb("ppb%d" % i, [128, 128], BF16) for i in range(4)]
    S_f = sb("S_f", [128, nl * 4, 128], F32)
    S_b = sb("S_b", [128, nl * 4, 128], BF16)
    osb = sb("osb", [128, TT], F32)
    ocb = sb("ocb", [128, TT], F32)
    onb = sb("onb", [128, TT], F32)
    xst = [sb("xst%d" % i, [128, D], F32) for i in range(2)]
    rst = [sb("rst%d" % i, [128, D], F32) for i in range(2)] if pipelined else None
    yst = [sb("yst%d" % i, [128, D], F32) for i in range(2)]
    cs_sb = [sb("cs%d" % i, [128, 2 * 4 * 64], F32) for i in range(2)]
    ring = [sb("ring%d" % i, [128, SLOT_E], BF16) for i in range(NS_RING)]
    wnat = sb("wnat", [128, 128], F32)
    wnm = sb("wnm", [128, 128], F32)
    wrep = sb("wrep", [128, 16], F32)
    browf = sb("browf", [1, 128], F32)
    if has_sample:
        s0f = [sb("s0f%d" % i, [128, 4, 128], F32) for i in range(2)]
        s0b = [sb("s0b%d" % i, [128, 4, 128], BF16) for i in range(2)]
        snew = [sb("snew%d" % i, [128, 4, 128], F32) for i in range(2)]
        kdm = [sb("kdm%d" % i, [128, TT], BF16) for i in range(2)]
    ps = [nc.alloc_psum_tensor("ps%d" % i, [128, 512], F32) for i in range(8)]

    sm = {}

    def dsem(name):
        if name not in sm:
            sm[name] = P.dma_sem("d_" + name)
        return sm[name]

    state = {"bank": 0, "ft": 0, "wpos": 0, "wiss": 0, "pp": 0, "q7": 0, "q6": 0}

    def bank():
        b = ps[state["bank"] % 6]
        state["bank"] += 1
        return b

    def ft():
        t = ftmp[state["ft"] % 4]
        state["ft"] += 1
        return t

    def quarter(which):
        k = state[which] % 4
        state[which] += 1
        return ps[7 if which == "q7" else 6][:, k * 128:(k + 1) * 128]

    ident = cf[:, C_ID:C_ID + 128]

    def vcol(i):
        return cf[:, C_VEC + i:C_VEC + i + 1]

    npc = len(_PIECES)
    wseq = []
    for t in range(nt):
        for l in range(nl):
            for pi in range(npc):
                wseq.append((l, pi))

    def wissue(upto):
        while state["wiss"] <= min(upto, len(wseq) - 1):
            q = state["wiss"]
            l, pi = wseq[q]
            E = _PIECE_E[pi]
            off = _PIECE_OFF[pi]
            slot = q % NS_RING
            gi = _GRP_OF[pi]
            lo = off - _GRPS[gi][0]
            P.dma(ring[slot][:, 0:E], wbf[l][gi][:, lo:lo + E], dsem("ring%d" % slot))
            state["wiss"] += 1

    def wnext(kind, held=0):
        p = state["wpos"]
        state["wiss"] = max(state["wiss"], p)
        l, pi = wseq[p]
        assert _PIECES[pi][0] == kind, (_PIECES[pi][0], kind)
        wissue(p + NS_RING - 1 - held)
        state["wpos"] += 1
        return ring[p % NS_RING]

    if do_cast:
        for l in range(nl):
            for gi, (g0, gn) in enumerate(_GRPS):
                c0 = 0
                while c0 < gn:
                    c1 = min(gn, c0 + 8192)
                    P.dma(wbf[l][gi][:, c0:c1], wf32[l][:, g0 + c0:g0 + c1], dsem("cast%d_%d" % (l, gi)), q="pool")
                    c0 = c1
    P.dma(cf[:, :], cst, dsem("cf"))
    P.copy("act", identb[:, :], ident)
    P.memset("dve", onesb[:, :], 1.0)
    P.memset("dve", S_f[:, :, :], 0.0)
    P.memset("dve", S_b[:, :, :], 0.0)
    for l in range(nl):
        for g in range(4):
            for kind in range(2):
                if (kind == 1 and not has_sample) or "gate" not in stages:
                    continue
                wi = (l * 2 + kind) * 4 + g
                if kind == 0:
                    P.dma(wnat[:, :], gws[l, g, :, :], dsem("wnat"))
                    P.tt("dve", wnm[:, :], wnat[:, :], cf[:, C_MPN:C_MPN + 128], ALU.mult)
                    P.dma(browf[0:1, :], gbs[l, g:g + 1, :], dsem("browf"))
                else:
                    base = (l * 4 + g) * 128 * 128
                    P.dma(wrep[:, :], bass.AP(gws.tensor, base, [[0, 8], [128, 16], [1, 16]]), dsem("wrep"))
                    P.tt("dve", wnm[:, :].rearrange("p (b j) -> p b j", b=8),
                         wrep[:, :].unsqueeze(1).broadcast_to([128, 8, 16]),
                         cf[:, C_MSN:C_MSN + 128].rearrange("p (b j) -> p b j", b=8), ALU.mult)
                    P.dma(browf[0:1, :].rearrange("p (b j) -> p b j", b=8),
                          bass.AP(gbs.tensor, (l * 4 + g) * 128, [[0, 1], [0, 8], [1, 16]]), dsem("browf"))
                q = quarter("q7")
                P.tr(q, wnm[:, :], ident)
                P.copy("act", wgT[:, wi, :], q)
                P.copy("act", brow[0:1, wi * 128:(wi + 1) * 128], browf[0:1, :])

    def rmsnorm(wbase, out_bf=True):
        st = ps[6]
        for c in range(8):
            s = sqb[c % 2]
            P.act(s[:, :], xres[:, c, :], AF.Square)
            P.mm(st[:, :], onesb[:, :], s[:, :], start=(c == 0), stop=(c == 7))
        P.act(sdt[:, :], st[:, :], AF.Sqrt, bias=EPS, scale=1.0 / D)
        P.recip(rstd[:, :], sdt[:, :])
        for c in range(8):
            dst = xn[:, c, :] if out_bf else xres[:, c, :]
            P.stt("dve", dst, xres[:, c, :], vcol(wbase + c), rstd[:, :], ALU.mult, ALU.mult)

    def ffn(wbase):
        for c in range(8):
            P.ts("dve", xn[:, c, :], xres[:, c, :], vcol(wbase + c), None, ALU.mult)
        st = ps[6]
        for pc in range(11):
            W = wnext("gu")[:, :].rearrange("p (k f c) -> p k f c", k=8, f=4)
            if pc == 0:
                fb = [bank() for _ in range(4)]
                for kc in range(8):
                    for f in range(4):
                        P.mm(fb[f][:, :], W[:, kc, f, :], xn[:, kc, :], start=(kc == 0), stop=(kc == 7))
                    sq = sqb[kc % 2]
                    P.act(sq[:, :], xres[:, kc, :], AF.Square)
                    P.mm(st[:, :], onesb[:, :], sq[:, :], start=(kc == 0), stop=(kc == 7))
                P.act(sdt[:, :], st[:, :], AF.Sqrt, bias=EPS, scale=1.0 / D)
                P.recip(rstd[:, :], sdt[:, :])
            for jj in range(2):
                j = 2 * pc + jj
                if pc == 0:
                    pa, pb = fb[jj * 2], fb[jj * 2 + 1]
                else:
                    pa = bank()
                    pb = bank()
                    for kc in range(8):
                        P.mm(pa[:, :], W[:, kc, jj * 2, :], xn[:, kc, :], start=(kc == 0), stop=(kc == 7))
                    for kc in range(8):
                        P.mm(pb[:, :], W[:, kc, jj * 2 + 1, :], xn[:, kc, :], start=(kc == 0), stop=(kc == 7))
                a1 = ft()
                b1 = ft()
                P.tt("dve", a1[:, :], pa[:, :], rstd[:, :], ALU.mult)
                P.tt("dve", b1[:, :], pb[:, :], rstd[:, :], ALU.mult)
                P.act(a1[:, :], a1[:, :], AF.Silu)
                P.tt("pool", gbuf[:, j, :], a1[:, :], b1[:, :], ALU.mult)
        for m in range(8):
            W = wnext("down")[:, 0:NJ * 128].rearrange("p (j c) -> p j c", j=NJ)
            acc = bank()
            for j in range(NJ):
                P.mm(acc[:, :], W[:, j, :], gbuf[:, j, :], start=(j == 0), stop=(j == NJ - 1))
            P.stt("dve", xres[:, m, :], acc[:, :], 0.5, xres[:, m, :], ALU.mult, ALU.add)

    def rotary(src, cs, blk, dst, slot):
        sv = src.rearrange("p (h t f) -> p h t f", h=4, t=2)
        dv = dst.rearrange("p (h t f) -> p h t f", h=4, t=2)
        x1 = sv[:, :, 0, :]
        x2 = sv[:, :, 1, :]
        csv = cs[:, :].rearrange("p (a b f) -> p a b f", a=2, b=4)
        cos = csv[:, 0, blk, :].unsqueeze(1).broadcast_to([128, 4, 64])
        sin = csv[:, 1, blk, :].unsqueeze(1).broadcast_to([128, 4, 64])
        r = rt[slot][:, :, :].rearrange("p a (h f) -> p a h f", h=4)
        P.tt("dve", r[:, 0], x1, cos, ALU.mult)
        P.tt("dve", r[:, 1], x2, sin, ALU.mult)
        P.tt("dve", dv[:, :, 0, :], r[:, 0], r[:, 1], ALU.subtract)
        P.tt("pool", r[:, 2], x1, sin, ALU.mult)
        P.tt("pool", r[:, 3], x2, cos, ALU.mult)
        P.tt("pool", dv[:, :, 1, :], r[:, 2], r[:, 3], ALU.add)

    def mixer(t, l, kind, wbase, cs, si):
        rmsnorm(wbase)
        P.dma(lnt[:, :], bass.AP(lnwb.tensor, l * 1024, [[0, 128], [1, 1024]]), dsem("lnt"))
        gb = gblk(kind)
        kofs = kind * 4
        pend = []

        def flush(keep):
            while len(pend) > keep:
                nm_, blk_, xb_ = pend.pop(0)
                tk = slice(blk_ * 128, (blk_ + 1) * 128)
                pT = bank()
                pTb = pT[:, :].bitcast(BF16)
                for h in range(4):
                    P.tr(pTb[:, h * 128:(h + 1) * 128], xb_[:, h * 128:(h + 1) * 128], identb[:, :], inc=(h == 3))
                P.copy("act", (qT if nm_ == "q" else kT)[:, :, tk], pTb[:, 0:512].rearrange("p (h t) -> p h t", h=4))

        cnt = 0
        for nm in ("q", "k", "vr", "vm"):
            W = wnext(nm)[:, :].rearrange("p (k c) -> p k c", k=8)
            pps = None
            if nm == "q":
                pps = [bank() for _ in range(4)]
                for kc in range(8):
                    for blk in range(4):
                        P.mm(pps[blk][:, :], xn[:, kc, blk * 128:(blk + 1) * 128], W[:, kc, :], start=(kc == 0), stop=(kc == 7))
            for blk in range(4):
                tok = slice(blk * 128, (blk + 1) * 128)
                if pps is not None:
                    pp = pps[blk]
                else:
                    pp = bank()
                    for kc in range(8):
                        P.mm(pp[:, :], xn[:, kc, tok], W[:, kc, :], start=(kc == 0), stop=(kc == 7))
                if nm in ("q", "k"):
                    f = par[cnt % 4]
                    xb = qb_tok[cnt % 4]
                    P.copy("act", f[:, :], pp[:, :])
                    rotary(f[:, :], cs, blk, xb[:, :], cnt % 2)
                    cnt += 1
                    if nm == "k":
                        P.tt("dve", kd_tok[:, blk, :].rearrange("p (h e) -> p h e", h=4),
                             xb[:, :].rearrange("p (h e) -> p h e", h=4),
                             cf[:, C_KDEC + kofs:C_KDEC + kofs + 4].unsqueeze(2).broadcast_to([128, 4, 128]), ALU.mult)
                    pend.append((nm, blk, xb))
                    flush(2)
                elif nm == "vr":
                    P.copy("act", v_tok[:, blk, :], pp[:, :])
                    flush(max(0, len(pend) - 1))
                else:
                    P.rec("dve", lambda e, pp=pp: e.bn_stats(st6[:, :], pp[:, :]), [pp[:, :]], [st6[:, :]])
                    P.rec("dve", lambda e: e.bn_aggr(mv[:, 0:2], st6[:, :]), [st6[:, :]], [mv[:, 0:2]])
                    P.act(mv[:, 2:3], mv[:, 1:2], AF.Sqrt, bias=EPS, scale=1.0)
                    P.recip(mv[:, 3:4], mv[:, 2:3])
                    z = ft()
                    P.ts("dve", z[:, :], pp[:, :], mv[:, 0:1], mv[:, 3:4], ALU.subtract, ALU.mult)
                    vf = vnf[blk % 2]
                    P.tt("pool", z[:, :], z[:, :], lnt[:, 0:512], ALU.mult)
                    P.tt("pool", vf[:, :], z[:, :], lnt[:, 512:1024], ALU.add)
                    P.copy("act", vn_tok[:, blk, :], vf[:, :])
                    if kind == 1:
                        P.dma(gvo[si, l, blk * 128:(blk + 1) * 128, :], vf[:, :], dsem("vnf%d" % (blk % 2)))
        flush(0)
        oT = [ps[h] for h in range(4)]
        mask = cf[:, (C_MP if kind == 0 else C_MS):(C_MP if kind == 0 else C_MS) + 128]
        for blk in range(4):
            tok = slice(blk * 128, (blk + 1) * 128)
            for h in range(4):
                hs = slice(h * 128, (h + 1) * 128)
                sc = ps[6 + (state["q6"] % 2)][:, 0:128]
                state["q6"] += 1
                P.mm(sc, kT[:, h, tok], qT[:, h, tok])
                pb_ = ppb[state["pp"] % 4]
                state["pp"] += 1
                P.stt("dve", pb_[:, :], sc, cf[:, C_GNEG + kofs + h:C_GNEG + kofs + h + 1], mask, ALU.mult, ALU.mult)
                P.mm(oT[h][:, tok], v_tok[:, blk, hs], pb_[:, :], start=True, stop=False, inc=False)
                if kind == 0:
                    P.mm(oT[h][:, tok], S_b[:, l * 4 + h, :], qT[:, h, tok], start=False, stop=True)
                    kv = ps[4 + (h % 2)][:, 0:128]
                    P.mm(kv, kd_tok[:, blk, hs], v_tok[:, blk, hs])
                    P.stt("dve", S_f[:, l * 4 + h, :], S_f[:, l * 4 + h, :], gb[h], kv, ALU.mult, ALU.add)
                    P.copy("act", S_b[:, l * 4 + h, :], S_f[:, l * 4 + h, :])
            if kind == 1:
                for s in range(8):
                    b = blk * 8 + s
                    sl = b % 2
                    P.dma(s0f[sl][:, :, :], sret[l, b].rearrange("h d e -> d h e"), dsem("s0f%d" % sl))
                    P.copy("act", s0b[sl][:, :, :], s0f[sl][:, :, :])
                    P.ts("dve", kdm[sl][:, :], kd_tok[:, blk, :], cf[:, C_SEQM + s:C_SEQM + s + 1], None, ALU.mult)
                    cols = slice(b * 16, (b + 1) * 16)
                    if "xA" in stages:
                        P.copy("dve", snew[sl][:, :, :], s0f[sl][:, :, :])
                    else:
                        for h in range(4):
                            hs = slice(h * 128, (h + 1) * 128)
                            if "xB" not in stages:
                                P.mm(oT[h][:, cols], s0b[sl][:, h, :], qT[:, h, cols], start=False, stop=(s == 7), inc=(s == 7))
                            kv = ps[4 + (h % 2)][:, 0:128]
                            P.mm(kv, kdm[sl][:, hs], v_tok[:, blk, hs])
                            P.stt("dve", snew[sl][:, h, :], s0f[sl][:, h, :], gb[h], kv, ALU.mult, ALU.add)
                    P.dma(sso[si, l, b].rearrange("h d e -> d h e"), snew[sl][:, :, :], dsem("snew%d" % sl))
        if kind == 0 and t >= npt - 2:
            P.dma(spo[t - (npt - 2), l].rearrange("h d e -> d h e"), S_f[:, l * 4:(l + 1) * 4, :], dsem("spo"))
        if "mD" not in stages:
            state["wpos"] += 12
            return
        Wg = wnext("gr")[:, :].rearrange("p (k c) -> p k c", k=8)
        Wu = wnext("u", held=1)[:, :].rearrange("p (k c) -> p k c", k=8)
        gpos0 = C_GPOS + kind * 512
        gnbase = nl * 24 + 8 + l * 4
        gate_ps = {}

        def gate_mm(g):
            gs = slice(g * 128, (g + 1) * 128)
            wi = (l * 2 + kind) * 4 + g
            sT = ps[4]
            for blk in range(4):
                tok = slice(blk * 128, (blk + 1) * 128)
                P.mm(sT[:, tok], vn_tok[:, blk, gs], wgT[:, wi, :], start=True, stop=False, inc=False)
                P.mm(sT[:, tok], onesb[0:1, :], brow[0:1, wi * 128:(wi + 1) * 128], start=False, stop=True,
                     inc=(blk == 3))
            uT = ps[5]
            for kc in range(8):
                P.mm(uT[:, :], Wu[:, kc, gs], xn[:, kc, :], start=(kc == 0), stop=(kc == 7))
            gate_ps[g] = (sT, uT)

        def gate_ev(g):
            sT, uT = gate_ps[g]
            ssb = ft()
            P.copy("act", ssb[:, :], sT[:, :])
            P.tt("dve", gated[:, g, :], ssb[:, :], uT[:, :], ALU.mult)

        def d1(h):
            gp = cf[:, gpos0 + h * 128:gpos0 + (h + 1) * 128].unsqueeze(1).broadcast_to([128, 4, 128])
            P.tt("dve", osb[:, :].rearrange("p (b i) -> p b i", b=4),
                 oT[h][:, :].rearrange("p (b i) -> p b i", b=4), gp, ALU.mult)
            P.copy("act", sqb[0][:, :], osb[:, :])
            P.mm(ps[6][:, :], onesb[:, :], sqb[0][:, :])

        def d2(h):
            P.stt("dve", ocb[:, :], ps[6][:, :], -1.0 / 128, osb[:, :], ALU.mult, ALU.add)
            P.act(sqb[1][:, :], ocb[:, :], AF.Square)
            P.mm(ps[7][:, :], onesb[:, :], sqb[1][:, :])

        def d3(h):
            hs = slice(h * 128, (h + 1) * 128)
            pg = ps[h]
            for kc in range(8):
                P.mm(pg[:, :], Wg[:, kc, hs], xn[:, kc, :], start=(kc == 0), stop=(kc == 7))
            P.act(sdt[:, :], ps[7][:, :], AF.Sqrt, bias=EPS, scale=1.0 / 128)
            P.recip(rstd[:, :], sdt[:, :])
            P.stt("dve", onb[:, :], ocb[:, :], vcol(gnbase + h), rstd[:, :], ALU.mult, ALU.mult)
            sg = ft()
            P.act(sg[:, :], pg[:, :], AF.Silu)
            P.tt("pool", og[:, h, :], onb[:, :], sg[:, :], ALU.mult)

        d1(0)
        for h in range(4):
            gate_mm(h)
            d2(h)
            gate_ev(h)
            d3(h)
            if h < 3:
                d1(h + 1)
        for m in range(8):
            W = wnext("merge")[:, 0:24 * 128].rearrange("p (b c) -> p b c", b=24)
            pa = bank()
            pm = bank()
            py = bank()
            pz = bank()
            for kc in range(8):
                P.mm(pa[:, :], W[:, kc, :], xn[:, kc, :], start=(kc == 0), stop=(kc == 7))
            for kc in range(8):
                P.mm(pm[:, :], W[:, 8 + kc, :], xn[:, kc, :], start=(kc == 0), stop=(kc == 7))
            for kc in range(4):
                P.mm(py[:, :], W[:, 16 + kc, :], og[:, kc, :], start=(kc == 0), stop=(kc == 3))
            for kc in range(4):
                P.mm(pz[:, :], W[:, 20 + kc, :], gated[:, kc, :], start=(kc == 0), stop=(kc == 3))
            sr = ft()
            sm_ = ft()
            P.act(sr[:, :], pa[:, :], AF.Sigmoid)
            P.act(sm_[:, :], pm[:, :], AF.Sigmoid)
            P.tt("dve", sr[:, :], sr[:, :], py[:, :], ALU.mult)
            P.tt("dve", sm_[:, :], sm_[:, :], pz[:, :], ALU.mult)
            P.tt("pool", merged[:, m, :], sr[:, :], sm_[:, :], ALU.add)
        for hf in range(2):
            W = wnext("wout")[:, :].rearrange("p (k c) -> p k c", k=8)
            for mm_ in range(4):
                m = hf * 4 + mm_
                acc = bank()
                for kc in range(8):
                    P.mm(acc[:, :], W[:, kc, mm_ * 128:(mm_ + 1) * 128], merged[:, kc, :], start=(kc == 0), stop=(kc == 7))
                P.tt("dve", xres[:, m, :], acc[:, :], xres[:, m, :], ALU.add)

    cc_sem = None
    if pipelined:
        cc_sem = Sem("cc", 1)
        P.dsems.append(cc_sem)
        P.memset("dve", rst[0][:, :], 0.0)
        for blk in range(4):
            P.dma(gath[blk * 128:(blk + 1) * 128, :], rst[0][:, :], dsem("rst0"))

    def store_tile(dst_rows):
        for blk in range(4):
            ys = yst[blk % 2]
            for hf in range(2):
                pT = bank()
                for cc in range(4):
                    c = hf * 4 + cc
                    P.tr(pT[:, cc * 128:(cc + 1) * 128], xres[:, c, blk * 128:(blk + 1) * 128], ident, inc=(cc == 3))
                P.copy("act" if hf == 0 else "dve", ys[:, hf * 512:(hf + 1) * 512], pT[:, :])
            P.dma(dst_rows(blk), ys[:, :], dsem("yst%d" % (blk % 2)))

    for t in range(nt):
        kind = 0 if t < npt else 1
        si = max(0, t - npt)
        cs = cs_sb[t % 2]
        P.dma(cs[:, :], cstab[t], dsem("cs%d" % (t % 2)))
        for blk in range(4):
            xs = xst[blk % 2]
            P.dma(xs[:, :], xin[t * TT + blk * 128:t * TT + (blk + 1) * 128, :], dsem("xst%d" % (blk % 2)))
            if pipelined:
                rs = rst[blk % 2]
                P.dma(rs[:, :], gath[blk * 128:(blk + 1) * 128, :], dsem("rst%d" % (blk % 2)))
                P.ts("dve", xs[:, :], xs[:, :], cf[:, C_MA:C_MA + 1], None, ALU.mult)
                P.stt("dve", xs[:, :], rs[:, :], cf[:, C_MA + 1:C_MA + 2], xs[:, :], ALU.mult, ALU.add)
            for hf in range(2):
                pT = bank()
                for cc in range(4):
                    c = hf * 4 + cc
                    P.tr(pT[:, cc * 128:(cc + 1) * 128], xs[:, c * 128:(c + 1) * 128], ident, inc=(cc == 3))
                P.copy("act" if hf == 0 else "dve", xres[:, hf * 4:(hf + 1) * 4, blk * 128:(blk + 1) * 128],
                       pT[:, :].rearrange("p (c t) -> p c t", c=4))
        for l in range(nl):
            if "ffn1" in stages:
                ffn(l * 24)
            else:
                state["wpos"] += 19
            if "mix" in stages:
                mixer(t, l, kind, l * 24 + 8, cs, si)
            else:
                state["wpos"] += 16
            if "ffn2" in stages:
                ffn(l * 24 + 16)
            else:
                state["wpos"] += 19
        if pipelined and t < nt - 1:
            store_tile(lambda blk: send[blk * 128:(blk + 1) * 128, :])
            P.rec("pool", lambda e: e.collective_compute("AllGather", ALU.bypass, replica_groups=groups,
                                                         ins=[send], outs=[gath]),
                  [send], [gath], csem=cc_sem)
        rmsnorm(nl * 24, out_bf=False)
        store_tile(lambda blk, t=t: yout[t * TT + blk * 128:t * TT + (blk + 1) * 128, :])
    assert state["wpos"] == len(wseq)
    P.emit()
    return nc, P


def kernel(x_prompt, x_sample, state_ret, ffn1_norm, ffn1_w_gu, ffn1_w_down, mix_norm, w_in, ret_gn_w,
           gmlp_ln_w, gmlp_ln_b, gmlp_ws, gmlp_bs, w_br_ret, w_br_mlp, w_out, ffn2_norm, ffn2_w_gu,
           ffn2_w_down, final_norm):
    w = dict(ffn1_norm=ffn1_norm, ffn1_w_gu=ffn1_w_gu, ffn1_w_down=ffn1_w_down, mix_norm=mix_norm, w_in=w_in,
             ret_gn_w=ret_gn_w, w_br_ret=w_br_ret, w_br_mlp=w_br_mlp, w_out=w_out, ffn2_norm=ffn2_norm,
             ffn2_w_gu=ffn2_w_gu, ffn2_w_down=ffn2_w_down, final_norm=final_norm)
    w = {k: np.asarray(v, dtype=np.float32) for k, v in w.items()}
    x_prompt = np.asarray(x_prompt, np.float32)
    x_sample = np.asarray(x_sample, np.float32)
    state_ret = np.ascontiguousarray(np.asarray(state_ret, np.float32))
    B, L, _ = x_prompt.shape
    nl = state_ret.shape[0]
    nb_s, ls = x_sample.shape[0], x_sample.shape[1]
    past = 4096
    npr = L // TT
    npt = npr + 1
    ns = 2
    nlc = nl // 2
    n_cores = 8
    groups = [[2 * p, 2 * p + 1] for p in range(n_cores // 2)]
    nc, _ = build_program(npt, nlc, ns, pipelined=True, groups=groups)
    xs_flat = x_sample.reshape(nb_s * ls, D)
    zt = np.zeros((TT, D), np.float32)
    pos_p = np.arange(L, dtype=np.float32).reshape(npr, TT)
    pos_s = (past + (np.arange(TT) % ls)).astype(np.float32)[None, :]
    pz = np.zeros((1, TT), np.float32)
    lnw = np.asarray(gmlp_ln_w, np.float32)
    lnb = np.asarray(gmlp_ln_b, np.float32)
    gws_all = np.asarray(gmlp_ws, np.float32)
    gbs_all = np.asarray(gmlp_bs, np.float32)
    role_in = []
    for role in range(2):
        lsl = slice(role * nlc, (role + 1) * nlc)
        wc = {k: (v if k == "final_norm" else v[lsl]) for k, v in w.items()}
        pos = np.concatenate([pos_p, pz, pos_s, pz] if role == 0 else [pz, pos_p, pz, pos_s], axis=0)
        role_in.append({
            "wf32": np.stack([pack_weights(wc, l) for l in range(nlc)]),
            "cst": build_consts(nlc, wc, m_a=1.0 if role == 0 else 0.0),
            "cstab": build_cstab(pos).reshape(npt + ns, 128, 512),
            "lnwb": np.ascontiguousarray(np.concatenate([lnw[lsl], lnb[lsl]], axis=1)),
            "gws": np.ascontiguousarray(gws_all[lsl]),
            "gbs": np.ascontiguousarray(gbs_all[lsl]),
            "sret": np.ascontiguousarray(state_ret[lsl]),
        })
    xin_b = np.zeros(((npt + ns) * TT, D), np.float32)
    in_maps = []
    for c in range(n_cores):
        p, role = c // 2, c % 2
        m = dict(role_in[role])
        if role == 0:
            m["xin"] = np.ascontiguousarray(np.concatenate([x_prompt[p % B], zt, xs_flat, zt], axis=0))
        else:
            m["xin"] = xin_b
        in_maps.append(m)
    res = run_bass_kernel_spmd(nc, in_maps, core_ids=list(range(n_cores)))
    r = res.results
    y_prompt = np.stack([r[2 * b + 1]["yout"][TT:TT + L] for b in range(B)]).astype(np.float32)
    y_sample = r[1]["yout"][(npt + 1) * TT:(npt + 2) * TT].reshape(nb_s, ls, D).astype(np.float32)
    sp = np.stack([np.concatenate([r[2 * b]["spo"][0], r[2 * b + 1]["spo"][1]], axis=0) for b in range(B)],
                  axis=1).astype(np.float32)
    ss = np.concatenate([r[0]["sso"][0], r[1]["sso"][1]], axis=0).astype(np.float32)
    gv = np.concatenate([r[0]["gvo"][0], r[1]["gvo"][1]], axis=0).reshape(nl, nb_s, ls, 512).astype(np.float32)
    return (y_prompt, y_sample, sp, ss, gv)
```

```python
import contextlib
import numpy as np
import concourse.bass as bass
import concourse.mybir as mybir
from concourse.bass_utils import run_bass_kernel_spmd

F32 = mybir.dt.float32
BF16 = mybir.dt.bfloat16
ALU = mybir.AluOpType
AF = mybir.ActivationFunctionType
_ESZ = {F32: 4, BF16: 2}

D = 1024
DFF = 2816
NJ = 22
NH = 4
TT = 512
EPS = 1e-6
WTOT = 192512
NS_RING = 5
SLOT_E = 4096


class Sem:
    __slots__ = ("name", "step", "h", "total")

    def __init__(self, name, step):
        self.name = name
        self.step = step
        self.h = None
        self.total = 0


class Op:
    __slots__ = ("q", "csem", "fn", "deps", "inc", "cnt", "sig", "idx")


def _region(ap):
    t = ap.tensor
    name = t.name
    sp = str(ap.space).upper()
    if "DRAM" in sp or "HBM" in sp:
        return name, 0, 1 << 60
    if "PSUM" in sp:
        return name.split("_bitcast")[0], 0, 2048
    F = 1
    for s in list(t.shape)[1:]:
        F *= int(s)
    esz = _ESZ.get(t.dtype, 4)
    off = int(ap.offset) % F
    ext = 0
    for (st, cn) in list(ap.ap)[1:]:
        ext += (int(cn) - 1) * abs(int(st))
    return name, off * esz, (off + ext + 1) * esz


class Prog:
    QUEUES = ("pe", "act", "dve", "pool", "sp")

    def __init__(self, nc):
        self.nc = nc
        self.ops = []
        self.qops = {q: [] for q in self.QUEUES}
        self.acc = {}
        self.esem = {q: Sem("s_" + q, 1) for q in ("pe", "act", "dve", "pool")}
        self.dsems = []
        self.pe_pending = []

    def dma_sem(self, name):
        s = Sem(name, 16)
        self.dsems.append(s)
        return s

    def rec(self, q, fn, reads, writes, csem=None, inc=True):
        op = Op()
        op.q = q
        op.fn = fn
        op.csem = csem if csem is not None else self.esem[q]
        op.inc = inc
        op.cnt = None
        op.sig = None
        idx = len(self.ops)
        op.idx = idx
        deps = set()
        csid = id(op.csem)
        for ap, is_w in [(r, False) for r in reads] + [(w, True) for w in writes]:
            name, lo, hi = _region(ap)
            d = self.acc.get(name)
            if d is None:
                d = {}
                self.acc[name] = d
            dead = []
            for key, j in d.items():
                l2, h2, cs2, w2 = key
                if j == idx:
                    continue
                if l2 < hi and lo < h2 and (is_w or w2):
                    deps.add(j)
                    if is_w and lo <= l2 and h2 <= hi:
                        dead.append(key)
            for k in dead:
                del d[k]
            d[(lo, hi, csid, is_w)] = idx
        real = set()
        for j in deps:
            o = self.ops[j]
            if q == "pe" and o.q == "pe":
                continue
            if not o.inc:
                if o.sig is None:
                    o.inc = True
                    k = self.pe_pending.index(o)
                    for p in self.pe_pending[: k + 1]:
                        p.sig = o
                    del self.pe_pending[: k + 1]
                real.add(o.sig.idx)
            else:
                real.add(j)
        op.deps = real
        self.ops.append(op)
        self.qops[q].append(op)
        if q == "pe":
            if inc:
                for p in self.pe_pending:
                    p.sig = op
                self.pe_pending = []
                op.sig = op
            else:
                self.pe_pending.append(op)
        else:
            op.sig = op
        return op

    def mm(self, out, lhsT, rhs, start=True, stop=True, inc=None):
        if inc is None:
            inc = stop
        return self.rec("pe", lambda e: e.matmul(out, lhsT, rhs, start=start, stop=stop),
                        [lhsT, rhs], [out], inc=inc)

    def tr(self, out, in_, ident, inc=True):
        return self.rec("pe", lambda e: e.transpose(out, in_, ident), [in_, ident], [out], inc=inc)

    def act(self, out, in_, func, bias=0.0, scale=1.0):
        rd = [in_]
        if not isinstance(bias, (int, float)):
            rd.append(bias)
        if not isinstance(scale, (int, float)):
            rd.append(scale)
        return self.rec("act", lambda e: e.activation(out, in_, func, bias=bias, scale=scale), rd, [out])

    def tt(self, q, out, in0, in1, op):
        return self.rec(q, lambda e: e.tensor_tensor(out, in0, in1, op), [in0, in1], [out])

    def ts(self, q, out, in0, s1, s2, op0, op1=ALU.bypass):
        rd = [in0]
        for s in (s1, s2):
            if s is not None and not isinstance(s, (int, float)):
                rd.append(s)
        if s2 is None:
            return self.rec(q, lambda e: e.tensor_scalar(out, in0, s1, None, op0), rd, [out])
        return self.rec(q, lambda e: e.tensor_scalar(out, in0, s1, s2, op0, op1), rd, [out])

    def stt(self, q, out, in0, scalar, in1, op0, op1):
        rd = [in0, in1]
        if not isinstance(scalar, (int, float)):
            rd.append(scalar)
        return self.rec(q, lambda e: e.scalar_tensor_tensor(out, in0, scalar, in1, op0, op1), rd, [out])

    def copy(self, q, out, in_):
        if q == "act":
            return self.rec(q, lambda e: e.activation(out, in_, AF.Copy), [in_], [out])
        return self.rec(q, lambda e: e.tensor_copy(out, in_), [in_], [out])

    def memset(self, q, out, val):
        return self.rec(q, lambda e: e.memset(out, val), [], [out])

    def recip(self, out, in_):
        return self.rec("dve", lambda e: e.reciprocal(out, in_), [in_], [out])

    def dma(self, out, in_, sem, q="sp"):
        return self.rec(q, lambda e: e.dma_start(out=out, in_=in_), [in_], [out], csem=sem)

    def emit(self):
        nc = self.nc
        for op in self.ops:
            if op.inc:
                op.csem.total += op.csem.step
                op.cnt = op.csem.total
        allsems = list(self.esem.values()) + self.dsems
        with contextlib.ExitStack() as st:
            for s in allsems:
                s.h = st.enter_context(nc.semaphore(s.name))
            block = st.enter_context(nc.Block())
            ops = self.ops

            def run(q, e):
                seen = {}
                for op in self.qops[q]:
                    need = {}
                    for j in op.deps:
                        o = ops[j]
                        cs = o.csem
                        c = o.cnt
                        if need.get(cs, 0) < c:
                            need[cs] = c
                    for cs, c in need.items():
                        if seen.get(cs, 0) >= c:
                            continue
                        seen[cs] = c
                        e.wait_ge(cs.h, c)
                    ins = op.fn(e)
                    if op.inc:
                        ins.then_inc(op.csem.h, op.csem.step)
                if q == "sp":
                    for s in self.dsems:
                        if s.total > 0:
                            e.wait_ge(s.h, s.total)
                    for s in self.esem.values():
                        if s.total > 0:
                            e.wait_ge(s.h, s.total)

            mp = {"pe": block.tensor, "act": block.scalar, "dve": block.vector,
                  "pool": block.gpsimd, "sp": block.sync}
            for q in self.QUEUES:
                if q == "sp" or self.qops[q]:
                    mp[q](lambda e, q=q: run(q, e))


def layer_pieces():
    pcs = []

    def ffn(tag):
        for pc in range(11):
            bl = []
            for kc in range(8):
                for jj in range(2):
                    j = 2 * pc + jj
                    bl.append((tag + "_w_gu", kc * 128, j * 128, 128))
                    bl.append((tag + "_w_gu", kc * 128, DFF + j * 128, 128))
            pcs.append(("gu", bl))
        for m in range(8):
            pcs.append(("down", [(tag + "_w_down", j * 128, m * 128, 128) for j in range(NJ)]))

    ffn("ffn1")
    for nm, c0 in (("q", 0), ("k", 512), ("vr", 1024), ("vm", 2560), ("gr", 1536), ("u", 2048)):
        pcs.append((nm, [("w_in", kc * 128, c0, 512) for kc in range(8)]))
    for m in range(8):
        bl = [("w_in", kc * 128, 3072 + m * 128, 128) for kc in range(8)]
        bl += [("w_in", kc * 128, 4096 + m * 128, 128) for kc in range(8)]
        bl += [("w_br_ret", kc * 128, m * 128, 128) for kc in range(4)]
        bl += [("w_br_mlp", kc * 128, m * 128, 128) for kc in range(4)]
        pcs.append(("merge", bl))
    for hf in range(2):
        pcs.append(("wout", [("w_out", kc * 128, hf * 512, 512) for kc in range(8)]))
    ffn("ffn2")
    return pcs


_PIECES = layer_pieces()
_PIECE_E = [sum(b[3] for b in bl) for (_, bl) in _PIECES]
_PIECE_OFF = [int(x) for x in np.concatenate([[0], np.cumsum(_PIECE_E)[:-1]])]
assert sum(_PIECE_E) == WTOT and max(_PIECE_E) <= SLOT_E
_GRP_OF = []
_GRPS = []
_g0, _gn = 0, 0
for _pi, _e in enumerate(_PIECE_E):
    if False:
        _GRPS.append((_g0, _gn))
        _g0, _gn = _g0 + _gn, 0
    _GRP_OF.append(len(_GRPS))
    _gn += _e
_GRPS.append((_g0, _gn))


def pack_weights(w, l):
    cols = []
    for (_, bl) in _PIECES:
        for (key, r0, c0, n) in bl:
            cols.append(w[key][l][r0:r0 + 128, c0:c0 + n])
    return np.ascontiguousarray(np.concatenate(cols, axis=1), dtype=np.float32)


C_ID = 0
C_MP = 128
C_MS = 256
C_MPN = 384
C_MSN = 512
C_GNEG = 640
C_KDEC = 648
C_SEQM = 656
C_GPOS = 664
C_VEC = 664 + 1024


def build_consts(nl, w, m_a=1.0):
    gam = (1.0 - 2.0 ** (-5.0 - np.arange(NH))).astype(np.float64)
    p = np.arange(128)
    nvec = nl * 3 * 8 + 8 + nl * 4 + 2
    c = np.zeros((128, C_VEC + nvec), np.float32)
    c[:, C_ID:C_ID + 128] = np.eye(128)
    jj, ii = np.meshgrid(p, p, indexing="ij")
    mp = (ii >= jj)
    ms = mp & ((ii // 16) == (jj // 16))
    c[:, C_MP:C_MP + 128] = mp
    c[:, C_MS:C_MS + 128] = ms
    c[:, C_MPN:C_MPN + 128] = mp.T
    c[:, C_MSN:C_MSN + 128] = ms.T
    sc = 128.0 ** -0.5
    for h in range(NH):
        c[:, C_GNEG + h] = sc * gam[h] ** (-(p + 1.0))
        c[:, C_GNEG + 4 + h] = sc * gam[h] ** (-((p % 16) + 1.0))
        c[:, C_KDEC + h] = sc * gam[h] ** (127.0 - p)
        c[:, C_KDEC + 4 + h] = sc * gam[h] ** (15.0 - (p % 16))
        c[:, C_GPOS + h * 128:C_GPOS + (h + 1) * 128] = (gam[h] ** (p + 1.0))[None, :]
        c[:, C_GPOS + 512 + h * 128:C_GPOS + 512 + (h + 1) * 128] = (gam[h] ** ((p % 16) + 1.0))[None, :]
    for s in range(8):
        c[:, C_SEQM + s] = (p // 16) == s
    o = C_VEC
    for l in range(nl):
        for i, key in enumerate(("ffn1_norm", "mix_norm", "ffn2_norm")):
            c[:, o:o + 8] = w[key][l].reshape(8, 128).T
            o += 8
    c[:, o:o + 8] = w["final_norm"].reshape(8, 128).T
    o += 8
    for l in range(nl):
        c[:, o:o + 4] = w["ret_gn_w"][l].reshape(4, 128).T
        o += 4
    c[:, o] = m_a
    c[:, o + 1] = 1.0 - m_a
    return c


def gblk(kind):
    gam = 1.0 - 2.0 ** (-5.0 - np.arange(NH))
    return [float(g ** (128.0 if kind == 0 else 16.0)) for g in gam]


def build_cstab(positions):
    freqs = (np.float32(10000.0) ** (-np.arange(64, dtype=np.float32) / np.float32(64))).astype(np.float32)
    ang = positions.astype(np.float32)[..., None] * freqs[None, None, :]
    cs = np.stack([np.cos(ang), np.sin(ang)], axis=1).astype(np.float32)
    nt = positions.shape[0]
    cs = cs.reshape(nt, 2, 4, 128, 64).transpose(0, 3, 1, 2, 4)
    return np.ascontiguousarray(cs)


def build_program(npt, nl, ns=1, pipelined=False, groups=None, do_cast=True,
                  stages=("gate", "ffn1", "mix", "mC", "mD", "mE", "mF", "ffn2")):
    nc = bass.Bass("TRN2", target_bir_lowering=False)
    P = Prog(nc)
    has_sample = ns > 0
    nt = npt + ns
    ntok = nt * TT
    nvec = nl * 3 * 8 + 8 + nl * 4 + 2
    NC = C_VEC + nvec
    C_MA = C_VEC + nvec - 2

    def din(name, shape, dt=F32):
        return nc.dram_tensor(name, shape, dt, kind="ExternalInput").ap()

    def dout(name, shape):
        return nc.dram_tensor(name, shape, F32, kind="ExternalOutput").ap()

    xin = din("xin", [ntok, D])
    wf32 = din("wf32", [nl, 128, WTOT])
    cst = din("cst", [128, NC])
    cstab = din("cstab", [nt, 128, 2 * 4 * 64])
    lnwb = din("lnwb", [nl, 2 * 512])
    gws = din("gws", [nl, 4, 128, 128])
    gbs = din("gbs", [nl, 4, 128])
    sret = din("sret", [nl, 32, 4, 128, 128])
    yout = dout("yout", [ntok, D])
    spo = dout("spo", [2, nl, 4, 128, 128])
    sso = dout("sso", [max(ns, 1), nl, 32, 4, 128, 128])
    gvo = dout("gvo", [max(ns, 1), nl, 512, 512])
    send = nc.dram_tensor("sendbuf", [TT, D], F32).ap()
    gath = nc.dram_tensor("gathbuf", [2 * TT, D], F32).ap()
    wbf = [[nc.dram_tensor("wbf%d_%d" % (l, gi), [128, gn], BF16).ap() for gi, (g0, gn) in enumerate(_GRPS)]
           for l in range(nl)]

    def sb(name, shape, dt):
        return nc.alloc_sbuf_tensor(name, shape, dt)

    cf = sb("cf", [128, NC], F32)
    identb = sb("identb", [128, 128], BF16)
    onesb = sb("onesb", [128, 128], BF16)
    wgT = sb("wgT", [128, nl * 2 * 4, 128], BF16)
    brow = sb("brow", [1, nl * 2 * 4 * 128], BF16)
    lnt = sb("lnt", [128, 1024], F32)
    xres = sb("xres", [128, 8, TT], F32)
    xn = sb("xn", [128, 8, TT], BF16)
    sqb = [sb("sqb%d" % i, [128, TT], BF16) for i in range(2)]
    sdt = sb("sdt", [128, TT], F32)
    rstd = sb("rstd", [128, TT], F32)
    U = sb("U", [128, NJ * TT], BF16)
    gbuf = U[:, :].rearrange("p (j t) -> p j t", j=NJ)
    qT = U[:, 0:4 * TT].rearrange("p (h t) -> p h t", h=4)
    kT = U[:, 4 * TT:8 * TT].rearrange("p (h t) -> p h t", h=4)
    kd_tok = U[:, 8 * TT:12 * TT].rearrange("p (b c) -> p b c", b=4)
    v_tok = U[:, 12 * TT:16 * TT].rearrange("p (b c) -> p b c", b=4)
    vn_tok = U[:, 16 * TT:20 * TT].rearrange("p (b c) -> p b c", b=4)
    merged = U[:, 0:8 * TT].rearrange("p (m t) -> p m t", m=8)
    og = sb("og", [128, 4, TT], BF16)
    gated = sb("gated", [128, 4, TT], BF16)
    ftmp = [sb("ftmp%d" % i, [128, TT], F32) for i in range(4)]
    rt = [sb("rt%d" % i, [128, 4, 256], F32) for i in range(2)]
    qb_tok = [sb("qbt%d" % i, [128, TT], BF16) for i in range(4)]
    par = [sb("par%d" % i, [128, TT], F32) for i in range(4)]
    vnf = [sb("vnf%d" % i, [128, TT], F32) for i in range(2)]
    st6 = sb("st6", [128, 6], F32)
    mv = sb("mv", [128, 4], F32)
    ppb = [sb("ppb%d" % i, [128, 128], BF16) for i in range(4)]
    S_f = sb("S_f", [128, nl * 4, 128], F32)
    S_b = sb("S_b", [128, nl * 4, 128], BF16)
    osb = sb("osb", [128, TT], F32)
    ocb = sb("ocb", [128, TT], F32)
    onb = sb("onb", [128, TT], F32)
    xst = [sb("xst%d" % i, [128, D], F32) for i in range(2)]
    rst = [sb("rst%d" % i, [128, D], F32) for i in range(2)] if pipelined else None
    yst = [sb("yst%d" % i, [128, D], F32) for i in range(2)]
    cs_sb = [sb("cs%d" % i, [128, 2 * 4 * 64], F32) for i in range(2)]
    ring = [sb("ring%d" % i, [128, SLOT_E], BF16) for i in range(NS_RING)]
    wnat = sb("wnat", [128, 128], F32)
    wnm = sb("wnm", [128, 128], F32)
    wrep = sb("wrep", [128, 16], F32)
    browf = sb("browf", [1, 128], F32)
    if has_sample:
        s0f = [sb("s0f%d" % i, [128, 4, 128], F32) for i in range(2)]
        s0b = [sb("s0b%d" % i, [128, 4, 128], BF16) for i in range(2)]
        snew = [sb("snew%d" % i, [128, 4, 128], F32) for i in range(2)]
        kdm = [sb("kdm%d" % i, [128, TT], BF16) for i in range(2)]
    ps = [nc.alloc_psum_tensor("ps%d" % i, [128, 512], F32) for i in range(8)]

    sm = {}

    def dsem(name):
        if name not in sm:
            sm[name] = P.dma_sem("d_" + name)
        return sm[name]

    state = {"bank": 0, "ft": 0, "wpos": 0, "wiss": 0, "pp": 0, "q7": 0, "q6": 0}

    _RING_BANKS = (0, 1, 2, 3, 4, 5, 7)

    def bank():
        b = ps[_RING_BANKS[state["bank"] % len(_RING_BANKS)]]
        state["bank"] += 1
        return b

    def ft():
        t = ftmp[state["ft"] % 4]
        state["ft"] += 1
        return t

    def quarter(which):
        k = state[which] % 4
        state[which] += 1
        return ps[7 if which == "q7" else 6][:, k * 128:(k + 1) * 128]

    ident = cf[:, C_ID:C_ID + 128]

    def vcol(i):
        return cf[:, C_VEC + i:C_VEC + i + 1]

    npc = len(_PIECES)
    wseq = []
    for t in range(nt):
        for l in range(nl):
            for pi in range(npc):
                wseq.append((l, pi))

    def wissue(upto):
        while state["wiss"] <= min(upto, len(wseq) - 1):
            q = state["wiss"]
            l, pi = wseq[q]
            E = _PIECE_E[pi]
            off = _PIECE_OFF[pi]
            slot = q % NS_RING
            gi = _GRP_OF[pi]
            lo = off - _GRPS[gi][0]
            P.dma(ring[slot][:, 0:E], wbf[l][gi][:, lo:lo + E], dsem("ring%d" % slot))
            state["wiss"] += 1

    def wnext(kind, held=0):
        p = state["wpos"]
        state["wiss"] = max(state["wiss"], p)
        l, pi = wseq[p]
        assert _PIECES[pi][0] == kind, (_PIECES[pi][0], kind)
        wissue(p + NS_RING - 1 - held)
        state["wpos"] += 1
        return ring[p % NS_RING]

    if do_cast:
        for l in range(nl):
            for gi, (g0, gn) in enumerate(_GRPS):
                c0 = 0
                while c0 < gn:
                    c1 = min(gn, c0 + 8192)
                    P.dma(wbf[l][gi][:, c0:c1], wf32[l][:, g0 + c0:g0 + c1], dsem("cast%d_%d" % (l, gi)), q="pool")
                    c0 = c1
    P.dma(cf[:, :], cst, dsem("cf"))
    P.copy("act", identb[:, :], ident)
    P.memset("dve", onesb[:, :], 1.0)
    P.memset("dve", S_f[:, :, :], 0.0)
    P.memset("dve", S_b[:, :, :], 0.0)
    for l in range(nl):
        for g in range(4):
            for kind in range(2):
                if (kind == 1 and not has_sample) or "gate" not in stages:
                    continue
                wi = (l * 2 + kind) * 4 + g
                if kind == 0:
                    P.dma(wnat[:, :], gws[l, g, :, :], dsem("wnat"))
                    P.tt("dve", wnm[:, :], wnat[:, :], cf[:, C_MPN:C_MPN + 128], ALU.mult)
                    P.dma(browf[0:1, :], gbs[l, g:g + 1, :], dsem("browf"))
                else:
                    base = (l * 4 + g) * 128 * 128
                    P.dma(wrep[:, :], bass.AP(gws.tensor, base, [[0, 8], [128, 16], [1, 16]]), dsem("wrep"))
                    P.tt("dve", wnm[:, :].rearrange("p (b j) -> p b j", b=8),
                         wrep[:, :].unsqueeze(1).broadcast_to([128, 8, 16]),
                         cf[:, C_MSN:C_MSN + 128].rearrange("p (b j) -> p b j", b=8), ALU.mult)
                    P.dma(browf[0:1, :].rearrange("p (b j) -> p b j", b=8),
                          bass.AP(gbs.tensor, (l * 4 + g) * 128, [[0, 1], [0, 8], [1, 16]]), dsem("browf"))
                q = quarter("q7")
                P.tr(q, wnm[:, :], ident)
                P.copy("act", wgT[:, wi, :], q)
                P.copy("act", brow[0:1, wi * 128:(wi + 1) * 128], browf[0:1, :])

    def rmsnorm(wbase, out_bf=True):
        st = ps[6]
        for c in range(8):
            s = sqb[c % 2]
            P.act(s[:, :], xres[:, c, :], AF.Square)
            P.mm(st[:, :], onesb[:, :], s[:, :], start=(c == 0), stop=(c == 7))
        P.act(sdt[:, :], st[:, :], AF.Sqrt, bias=EPS, scale=1.0 / D)
        P.recip(rstd[:, :], sdt[:, :])
        for c in range(8):
            dst = xn[:, c, :] if out_bf else xres[:, c, :]
            P.stt("dve", dst, xres[:, c, :], vcol(wbase + c), rstd[:, :], ALU.mult, ALU.mult)

    def ffn(wbase):
        for c in range(8):
            P.ts("dve", xn[:, c, :], xres[:, c, :], vcol(wbase + c), None, ALU.mult)
        st = ps[6]
        for pc in range(11):
            W = wnext("gu")[:, :].rearrange("p (k f c) -> p k f c", k=8, f=4)
            if pc == 0:
                fb = [bank() for _ in range(4)]
                for kc in range(8):
                    for f in range(4):
                        P.mm(fb[f][:, :], W[:, kc, f, :], xn[:, kc, :], start=(kc == 0), stop=(kc == 7))
                    sq = sqb[kc % 2]
                    P.act(sq[:, :], xres[:, kc, :], AF.Square)
                    P.mm(st[:, :], onesb[:, :], sq[:, :], start=(kc == 0), stop=(kc == 7))
                P.act(sdt[:, :], st[:, :], AF.Sqrt, bias=EPS, scale=1.0 / D)
                P.recip(rstd[:, :], sdt[:, :])
            for jj in range(2):
                j = 2 * pc + jj
                if pc == 0:
                    pa, pb = fb[jj * 2], fb[jj * 2 + 1]
                else:
                    pa = bank()
                    pb = bank()
                    for kc in range(8):
                        P.mm(pa[:, :], W[:, kc, jj * 2, :], xn[:, kc, :], start=(kc == 0), stop=(kc == 7))
                    for kc in range(8):
                        P.mm(pb[:, :], W[:, kc, jj * 2 + 1, :], xn[:, kc, :], start=(kc == 0), stop=(kc == 7))
                a1 = ft()
                b1 = ft()
                P.tt("dve", a1[:, :], pa[:, :], rstd[:, :], ALU.mult)
                P.tt("dve", b1[:, :], pb[:, :], rstd[:, :], ALU.mult)
                P.act(a1[:, :], a1[:, :], AF.Silu)
                P.tt("pool", gbuf[:, j, :], a1[:, :], b1[:, :], ALU.mult)
        for m in range(8):
            W = wnext("down")[:, 0:NJ * 128].rearrange("p (j c) -> p j c", j=NJ)
            acc = bank()
            for j in range(NJ):
                P.mm(acc[:, :], W[:, j, :], gbuf[:, j, :], start=(j == 0), stop=(j == NJ - 1))
            P.stt("dve", xres[:, m, :], acc[:, :], 0.5, xres[:, m, :], ALU.mult, ALU.add)

    def rotary(src, cs, blk, dst, slot):
        sv = src.rearrange("p (h t f) -> p h t f", h=4, t=2)
        dv = dst.rearrange("p (h t f) -> p h t f", h=4, t=2)
        x1 = sv[:, :, 0, :]
        x2 = sv[:, :, 1, :]
        csv = cs[:, :].rearrange("p (a b f) -> p a b f", a=2, b=4)
        cos = csv[:, 0, blk, :].unsqueeze(1).broadcast_to([128, 4, 64])
        sin = csv[:, 1, blk, :].unsqueeze(1).broadcast_to([128, 4, 64])
        r = rt[slot][:, :, :].rearrange("p a (h f) -> p a h f", h=4)
        P.tt("dve", r[:, 0], x1, cos, ALU.mult)
        P.tt("dve", r[:, 1], x2, sin, ALU.mult)
        P.tt("dve", dv[:, :, 0, :], r[:, 0], r[:, 1], ALU.subtract)
        P.tt("pool", r[:, 2], x1, sin, ALU.mult)
        P.tt("pool", r[:, 3], x2, cos, ALU.mult)
        P.tt("pool", dv[:, :, 1, :], r[:, 2], r[:, 3], ALU.add)

    def mixer(t, l, kind, wbase, cs, si):
        rmsnorm(wbase)
        P.dma(lnt[:, :], bass.AP(lnwb.tensor, l * 1024, [[0, 128], [1, 1024]]), dsem("lnt"))
        gb = gblk(kind)
        kofs = kind * 4
        pend = []

        def flush(keep):
            while len(pend) > keep:
                nm_, blk_, xb_ = pend.pop(0)
                tk = slice(blk_ * 128, (blk_ + 1) * 128)
                pT = bank()
                pTb = pT[:, :].bitcast(BF16)
                for h in range(4):
                    P.tr(pTb[:, h * 128:(h + 1) * 128], xb_[:, h * 128:(h + 1) * 128], identb[:, :], inc=(h == 3))
                P.copy("act", (qT if nm_ == "q" else kT)[:, :, tk], pTb[:, 0:512].rearrange("p (h t) -> p h t", h=4))

        cnt = 0
        for nm in ("q", "k", "vr", "vm"):
            W = wnext(nm)[:, :].rearrange("p (k c) -> p k c", k=8)
            pps = None
            if nm == "q":
                pps = [bank() for _ in range(4)]
                for kc in range(8):
                    for blk in range(4):
                        P.mm(pps[blk][:, :], xn[:, kc, blk * 128:(blk + 1) * 128], W[:, kc, :], start=(kc == 0), stop=(kc == 7))
            for blk in range(4):
                tok = slice(blk * 128, (blk + 1) * 128)
                if pps is not None:
                    pp = pps[blk]
                else:
                    pp = bank()
                    for kc in range(8):
                        P.mm(pp[:, :], xn[:, kc, tok], W[:, kc, :], start=(kc == 0), stop=(kc == 7))
                if nm in ("q", "k"):
                    f = par[cnt % 4]
                    xb = qb_tok[cnt % 4]
                    P.copy("act", f[:, :], pp[:, :])
                    rotary(f[:, :], cs, blk, xb[:, :], cnt % 2)
                    cnt += 1
                    if nm == "k":
                        P.tt("dve", kd_tok[:, blk, :].rearrange("p (h e) -> p h e", h=4),
                             xb[:, :].rearrange("p (h e) -> p h e", h=4),
                             cf[:, C_KDEC + kofs:C_KDEC + kofs + 4].unsqueeze(2).broadcast_to([128, 4, 128]), ALU.mult)
                    pend.append((nm, blk, xb))
                    flush(2)
                elif nm == "vr":
                    P.copy("act", v_tok[:, blk, :], pp[:, :])
                    flush(max(0, len(pend) - 1))
                else:
                    P.rec("dve", lambda e, pp=pp: e.bn_stats(st6[:, :], pp[:, :]), [pp[:, :]], [st6[:, :]])
                    P.rec("dve", lambda e: e.bn_aggr(mv[:, 0:2], st6[:, :]), [st6[:, :]], [mv[:, 0:2]])
                    P.act(mv[:, 2:3], mv[:, 1:2], AF.Sqrt, bias=EPS, scale=1.0)
                    P.recip(mv[:, 3:4], mv[:, 2:3])
                    z = ft()
                    P.ts("dve", z[:, :], pp[:, :], mv[:, 0:1], mv[:, 3:4], ALU.subtract, ALU.mult)
                    vf = vnf[blk % 2]
                    P.tt("pool", z[:, :], z[:, :], lnt[:, 0:512], ALU.mult)
                    P.tt("pool", vf[:, :], z[:, :], lnt[:, 512:1024], ALU.add)
                    P.copy("act", vn_tok[:, blk, :], vf[:, :])
                    if kind == 1:
                        P.dma(gvo[si, l, blk * 128:(blk + 1) * 128, :], vf[:, :], dsem("vnf%d" % (blk % 2)))
        flush(0)
        oT = [ps[h] for h in range(4)]
        mask = cf[:, (C_MP if kind == 0 else C_MS):(C_MP if kind == 0 else C_MS) + 128]
        for blk in range(4):
            tok = slice(blk * 128, (blk + 1) * 128)
            for h in range(4):
                hs = slice(h * 128, (h + 1) * 128)
                sc = ps[6 + (state["q6"] % 2)][:, 0:128]
                state["q6"] += 1
                P.mm(sc, kT[:, h, tok], qT[:, h, tok])
                pb_ = ppb[state["pp"] % 4]
                state["pp"] += 1
                P.stt("dve", pb_[:, :], sc, cf[:, C_GNEG + kofs + h:C_GNEG + kofs + h + 1], mask, ALU.mult, ALU.mult)
                P.mm(oT[h][:, tok], v_tok[:, blk, hs], pb_[:, :], start=True, stop=False, inc=False)
                if kind == 0:
                    P.mm(oT[h][:, tok], S_b[:, l * 4 + h, :], qT[:, h, tok], start=False, stop=True)
                    kv = ps[4 + (h % 2)][:, 0:128]
                    P.mm(kv, kd_tok[:, blk, hs], v_tok[:, blk, hs])
                    P.stt("dve", S_f[:, l * 4 + h, :], S_f[:, l * 4 + h, :], gb[h], kv, ALU.mult, ALU.add)
                    P.copy("act", S_b[:, l * 4 + h, :], S_f[:, l * 4 + h, :])
            if kind == 1:
                for s in range(8):
                    b = blk * 8 + s
                    sl = b % 2
                    P.dma(s0f[sl][:, :, :], sret[l, b].rearrange("h d e -> d h e"), dsem("s0f%d" % sl))
                    P.copy("act", s0b[sl][:, :, :], s0f[sl][:, :, :])
                    P.ts("dve", kdm[sl][:, :], kd_tok[:, blk, :], cf[:, C_SEQM + s:C_SEQM + s + 1], None, ALU.mult)
                    cols = slice(b * 16, (b + 1) * 16)
                    if "xA" in stages:
                        P.copy("dve", snew[sl][:, :, :], s0f[sl][:, :, :])
                    else:
                        for h in range(4):
                            hs = slice(h * 128, (h + 1) * 128)
                            if "xB" not in stages:
                                P.mm(oT[h][:, cols], s0b[sl][:, h, :], qT[:, h, cols], start=False, stop=(s == 7), inc=(s == 7))
                            kv = ps[4 + (h % 2)][:, 0:128]
                            P.mm(kv, kdm[sl][:, hs], v_tok[:, blk, hs])
                            P.stt("dve", snew[sl][:, h, :], s0f[sl][:, h, :], gb[h], kv, ALU.mult, ALU.add)
                    P.dma(sso[si, l, b].rearrange("h d e -> d h e"), snew[sl][:, :, :], dsem("snew%d" % sl))
        if kind == 0 and t >= npt - 2:
            P.dma(spo[t - (npt - 2), l].rearrange("h d e -> d h e"), S_f[:, l * 4:(l + 1) * 4, :], dsem("spo"))
        if "mD" not in stages:
            state["wpos"] += 12
            return
        Wg = wnext("gr")[:, :].rearrange("p (k c) -> p k c", k=8)
        Wu = wnext("u", held=1)[:, :].rearrange("p (k c) -> p k c", k=8)
        gpos0 = C_GPOS + kind * 512
        gnbase = nl * 24 + 8 + l * 4
        gate_ps = {}

        def gate_mm(g):
            gs = slice(g * 128, (g + 1) * 128)
            wi = (l * 2 + kind) * 4 + g
            sT = ps[4]
            for blk in range(4):
                tok = slice(blk * 128, (blk + 1) * 128)
                P.mm(sT[:, tok], vn_tok[:, blk, gs], wgT[:, wi, :], start=True, stop=False, inc=False)
                P.mm(sT[:, tok], onesb[0:1, :], brow[0:1, wi * 128:(wi + 1) * 128], start=False, stop=True,
                     inc=(blk == 3))
            uT = ps[5]
            for kc in range(8):
                P.mm(uT[:, :], Wu[:, kc, gs], xn[:, kc, :], start=(kc == 0), stop=(kc == 7))
            gate_ps[g] = (sT, uT)

        def gate_ev(g):
            sT, uT = gate_ps[g]
            ssb = ft()
            P.copy("act", ssb[:, :], sT[:, :])
            P.tt("dve", gated[:, g, :], ssb[:, :], uT[:, :], ALU.mult)

        def d1(h):
            gp = cf[:, gpos0 + h * 128:gpos0 + (h + 1) * 128].unsqueeze(1).broadcast_to([128, 4, 128])
            P.tt("dve", osb[:, :].rearrange("p (b i) -> p b i", b=4),
                 oT[h][:, :].rearrange("p (b i) -> p b i", b=4), gp, ALU.mult)
            P.copy("act", sqb[0][:, :], osb[:, :])
            P.mm(ps[6][:, :], onesb[:, :], sqb[0][:, :])

        def d2(h):
            P.stt("dve", ocb[:, :], ps[6][:, :], -1.0 / 128, osb[:, :], ALU.mult, ALU.add)
            P.act(sqb[1][:, :], ocb[:, :], AF.Square)
            P.mm(ps[7][:, :], onesb[:, :], sqb[1][:, :])

        def d3(h):
            hs = slice(h * 128, (h + 1) * 128)
            pg = ps[h]
            for kc in range(8):
                P.mm(pg[:, :], Wg[:, kc, hs], xn[:, kc, :], start=(kc == 0), stop=(kc == 7))
            P.act(sdt[:, :], ps[7][:, :], AF.Sqrt, bias=EPS, scale=1.0 / 128)
            P.recip(rstd[:, :], sdt[:, :])
            P.stt("dve", onb[:, :], ocb[:, :], vcol(gnbase + h), rstd[:, :], ALU.mult, ALU.mult)
            sg = ft()
            P.act(sg[:, :], pg[:, :], AF.Silu)
            P.tt("pool", og[:, h, :], onb[:, :], sg[:, :], ALU.mult)

        d1(0)
        for h in range(4):
            gate_mm(h)
            d2(h)
            gate_ev(h)
            d3(h)
            if h < 3:
                d1(h + 1)
        for m in range(8):
            W = wnext("merge")[:, 0:24 * 128].rearrange("p (b c) -> p b c", b=24)
            pa = bank()
            pm = bank()
            py = bank()
            pz = bank()
            for kc in range(8):
                P.mm(pa[:, :], W[:, kc, :], xn[:, kc, :], start=(kc == 0), stop=(kc == 7))
            for kc in range(8):
                P.mm(pm[:, :], W[:, 8 + kc, :], xn[:, kc, :], start=(kc == 0), stop=(kc == 7))
            for kc in range(4):
                P.mm(py[:, :], W[:, 16 + kc, :], og[:, kc, :], start=(kc == 0), stop=(kc == 3))
            for kc in range(4):
                P.mm(pz[:, :], W[:, 20 + kc, :], gated[:, kc, :], start=(kc == 0), stop=(kc == 3))
            sr = ft()
            sm_ = ft()
            P.act(sr[:, :], pa[:, :], AF.Sigmoid)
            P.act(sm_[:, :], pm[:, :], AF.Sigmoid)
            P.tt("dve", sr[:, :], sr[:, :], py[:, :], ALU.mult)
            P.tt("dve", sm_[:, :], sm_[:, :], pz[:, :], ALU.mult)
            P.tt("pool", merged[:, m, :], sr[:, :], sm_[:, :], ALU.add)
        for hf in range(2):
            W = wnext("wout")[:, :].rearrange("p (k c) -> p k c", k=8)
            for mm_ in range(4):
                m = hf * 4 + mm_
                acc = bank()
                for kc in range(8):
                    P.mm(acc[:, :], W[:, kc, mm_ * 128:(mm_ + 1) * 128], merged[:, kc, :], start=(kc == 0), stop=(kc == 7))
                P.tt("dve", xres[:, m, :], acc[:, :], xres[:, m, :], ALU.add)

    cc_sem = None
    if pipelined:
        cc_sem = Sem("cc", 1)
        P.dsems.append(cc_sem)
        P.memset("dve", rst[0][:, :], 0.0)
        for blk in range(4):
            P.dma(gath[blk * 128:(blk + 1) * 128, :], rst[0][:, :], dsem("rst0"))

    def store_tile(dst_rows):
        for blk in range(4):
            ys = yst[blk % 2]
            for hf in range(2):
                pT = bank()
                for cc in range(4):
                    c = hf * 4 + cc
                    P.tr(pT[:, cc * 128:(cc + 1) * 128], xres[:, c, blk * 128:(blk + 1) * 128], ident, inc=(cc == 3))
                P.copy("act" if hf == 0 else "dve", ys[:, hf * 512:(hf + 1) * 512], pT[:, :])
            P.dma(dst_rows(blk), ys[:, :], dsem("yst%d" % (blk % 2)))

    for t in range(nt):
        kind = 0 if t < npt else 1
        si = max(0, t - npt)
        cs = cs_sb[t % 2]
        P.dma(cs[:, :], cstab[t], dsem("cs%d" % (t % 2)))
        for blk in range(4):
            xs = xst[blk % 2]
            P.dma(xs[:, :], xin[t * TT + blk * 128:t * TT + (blk + 1) * 128, :], dsem("xst%d" % (blk % 2)))
            if pipelined:
                rs = rst[blk % 2]
                P.dma(rs[:, :], gath[blk * 128:(blk + 1) * 128, :], dsem("rst%d" % (blk % 2)))
                P.ts("dve", xs[:, :], xs[:, :], cf[:, C_MA:C_MA + 1], None, ALU.mult)
                P.stt("dve", xs[:, :], rs[:, :], cf[:, C_MA + 1:C_MA + 2], xs[:, :], ALU.mult, ALU.add)
            for hf in range(2):
                pT = bank()
                for cc in range(4):
                    c = hf * 4 + cc
                    P.tr(pT[:, cc * 128:(cc + 1) * 128], xs[:, c * 128:(c + 1) * 128], ident, inc=(cc == 3))
                P.copy("act" if hf == 0 else "dve", xres[:, hf * 4:(hf + 1) * 4, blk * 128:(blk + 1) * 128],
                       pT[:, :].rearrange("p (c t) -> p c t", c=4))
        for l in range(nl):
            if "ffn1" in stages:
                ffn(l * 24)
            else:
                state["wpos"] += 19
            if "mix" in stages:
                mixer(t, l, kind, l * 24 + 8, cs, si)
            else:
                state["wpos"] += 16
            if "ffn2" in stages:
                ffn(l * 24 + 16)
            else:
                state["wpos"] += 19
        if pipelined and t < nt - 1:
            store_tile(lambda blk: send[blk * 128:(blk + 1) * 128, :])
            P.rec("pool", lambda e: e.collective_compute("AllGather", ALU.bypass, replica_groups=groups,
                                                         ins=[send], outs=[gath]),
                  [send], [gath], csem=cc_sem)
        rmsnorm(nl * 24, out_bf=False)
        store_tile(lambda blk, t=t: yout[t * TT + blk * 128:t * TT + (blk + 1) * 128, :])
    assert state["wpos"] == len(wseq)
    P.emit()
    return nc, P


def kernel(x_prompt, x_sample, state_ret, ffn1_norm, ffn1_w_gu, ffn1_w_down, mix_norm, w_in, ret_gn_w,
           gmlp_ln_w, gmlp_ln_b, gmlp_ws, gmlp_bs, w_br_ret, w_br_mlp, w_out, ffn2_norm, ffn2_w_gu,
           ffn2_w_down, final_norm):
    w = dict(ffn1_norm=ffn1_norm, ffn1_w_gu=ffn1_w_gu, ffn1_w_down=ffn1_w_down, mix_norm=mix_norm, w_in=w_in,
             ret_gn_w=ret_gn_w, w_br_ret=w_br_ret, w_br_mlp=w_br_mlp, w_out=w_out, ffn2_norm=ffn2_norm,
             ffn2_w_gu=ffn2_w_gu, ffn2_w_down=ffn2_w_down, final_norm=final_norm)
    w = {k: np.asarray(v, dtype=np.float32) for k, v in w.items()}
    x_prompt = np.asarray(x_prompt, np.float32)
    x_sample = np.asarray(x_sample, np.float32)
    state_ret = np.ascontiguousarray(np.asarray(state_ret, np.float32))
    B, L, _ = x_prompt.shape
    nl = state_ret.shape[0]
    nb_s, ls = x_sample.shape[0], x_sample.shape[1]
    past = 4096
    npr = L // TT
    npt = npr + 1
    ns = 2
    nlc = nl // 2
    n_cores = 8
    groups = [[2 * p, 2 * p + 1] for p in range(n_cores // 2)]
    nc, _ = build_program(npt, nlc, ns, pipelined=True, groups=groups)
    xs_flat = x_sample.reshape(nb_s * ls, D)
    zt = np.zeros((TT, D), np.float32)
    pos_p = np.arange(L, dtype=np.float32).reshape(npr, TT)
    pos_s = (past + (np.arange(TT) % ls)).astype(np.float32)[None, :]
    pz = np.zeros((1, TT), np.float32)
    lnw = np.asarray(gmlp_ln_w, np.float32)
    lnb = np.asarray(gmlp_ln_b, np.float32)
    gws_all = np.asarray(gmlp_ws, np.float32)
    gbs_all = np.asarray(gmlp_bs, np.float32)
    role_in = []
    for role in range(2):
        lsl = slice(role * nlc, (role + 1) * nlc)
        wc = {k: (v if k == "final_norm" else v[lsl]) for k, v in w.items()}
        pos = np.concatenate([pos_p, pz, pos_s, pz] if role == 0 else [pz, pos_p, pz, pos_s], axis=0)
        role_in.append({
            "wf32": np.stack([pack_weights(wc, l) for l in range(nlc)]),
            "cst": build_consts(nlc, wc, m_a=1.0 if role == 0 else 0.0),
            "cstab": build_cstab(pos).reshape(npt + ns, 128, 512),
            "lnwb": np.ascontiguousarray(np.concatenate([lnw[lsl], lnb[lsl]], axis=1)),
            "gws": np.ascontiguousarray(gws_all[lsl]),
            "gbs": np.ascontiguousarray(gbs_all[lsl]),
            "sret": np.ascontiguousarray(state_ret[lsl]),
        })
    xin_b = np.zeros(((npt + ns) * TT, D), np.float32)
    in_maps = []
    for c in range(n_cores):
        p, role = c // 2, c % 2
        m = dict(role_in[role])
        if role == 0:
            m["xin"] = np.ascontiguousarray(np.concatenate([x_prompt[p % B], zt, xs_flat, zt], axis=0))
        else:
            m["xin"] = xin_b
        in_maps.append(m)
    res = run_bass_kernel_spmd(nc, in_maps, core_ids=list(range(n_cores)))
    r = res.results
    y_prompt = np.stack([r[2 * b + 1]["yout"][TT:TT + L] for b in range(B)]).astype(np.float32)
    y_sample = r[1]["yout"][(npt + 1) * TT:(npt + 2) * TT].reshape(nb_s, ls, D).astype(np.float32)
    sp = np.stack([np.concatenate([r[2 * b]["spo"][0], r[2 * b + 1]["spo"][1]], axis=0) for b in range(B)],
                  axis=1).astype(np.float32)
    ss = np.concatenate([r[0]["sso"][0], r[1]["sso"][1]], axis=0).astype(np.float32)
    gv = np.concatenate([r[0]["gvo"][0], r[1]["gvo"][1]], axis=0).reshape(nl, nb_s, ls, 512).astype(np.float32)
    return (y_prompt, y_sample, sp, ss, gv)
```
